# Optimizing a Trainium2 kernel written in Bass

```python
import jax, jax.numpy as jnp
from jax import lax
import numpy as np

D_MODEL = 1024
BATCH = 8
SEQ = 8192
DEPTH = 2

CHUNK = 64
HEAD_DIM = 64
SGU_BLOCK = 128
A_GROUPS = 4
A_WIDTH = A_GROUPS * HEAD_DIM
B_HEADS = 6
B_WIDTH = B_HEADS * HEAD_DIM
DECAY_LORA = 64
AAA_LORA = 64
MV_LORA = 32
C_HEADS = 6
C_WIDTH = C_HEADS * HEAD_DIM
LEFT_CHUNKS = 8
BAND = (LEFT_CHUNKS + 1) * CHUNK
REL_CLIP = 256
MIX_WIDTH = A_WIDTH + B_WIDTH + C_WIDTH
A_COLS = 3 * A_WIDTH
B_SHIFT_COLS = 3 * B_WIDTH + DECAY_LORA + AAA_LORA
B_COLS = B_SHIFT_COLS + B_WIDTH
C_COLS = 4 * C_WIDTH
PROJ_COLS = A_COLS + B_COLS + C_COLS
DEEPNORM_ALPHA = (2 * DEPTH) ** 0.25
DEEPNORM_BETA = (8 * DEPTH) ** -0.25
LN_EPS = 1e-5
GN_EPS = 64e-5

kernel_name = 'hybrid_sgu_rwkv7_chunkattn_deepnorm'


def layer_norm(x, g, b, eps=LN_EPS):
    xf = x.astype(jnp.float32)
    mu = jnp.mean(xf, axis=-1, keepdims=True)
    var = jnp.mean(jnp.square(xf - mu), axis=-1, keepdims=True)
    return ((xf - mu) * lax.rsqrt(var + eps)).astype(x.dtype) * g + b


def token_shift(p):
    return jnp.pad(p, ((0, 0), (1, 0), (0, 0)))[:, :-1]


def spatial_gating(u, v, ln_g, ln_b, w_s, b_s):
    bsz, t, _ = u.shape
    nb = t // SGU_BLOCK
    v = layer_norm(v.reshape(bsz, t, A_GROUPS, HEAD_DIM), ln_g.reshape(A_GROUPS, HEAD_DIM), ln_b.reshape(A_GROUPS, HEAD_DIM))
    v = v.reshape(bsz, nb, SGU_BLOCK, A_GROUPS, HEAD_DIM)
    mask = jnp.tril(jnp.ones((SGU_BLOCK, SGU_BLOCK), dtype=bool))
    w = jnp.where(mask[None], w_s, jnp.zeros_like(w_s))
    s = jnp.einsum('gpq,bnqgc->bnpgc', w, v) + b_s.T[None, None, :, :, None]
    return u * s.reshape(bsz, t, A_WIDTH)


def rwkv7_time_mix(p, mu, w0, w2, a0, a2, k_k, k_a, r_k, lnx_g, lnx_b, v_first, vres):
    bsz, t, _ = p.shape
    xs = p + (token_shift(p) - p) * mu
    r, k, v, w_lo, a_lo = jnp.split(xs, [B_WIDTH, 2 * B_WIDTH, 3 * B_WIDTH, 3 * B_WIDTH + DECAY_LORA], axis=-1)
    v_raw = v
    w = -jax.nn.softplus(-(w0 + jnp.tanh(w_lo) @ w2)) - 0.5
    decay = jnp.exp(-jnp.exp(w.astype(jnp.float32)))
    if vres is not None:
        v0, v1, v2 = vres
        v = v + (v_first - v) * jax.nn.sigmoid(v0 + (v @ v1) @ v2)
    a = jax.nn.sigmoid(a0 + a_lo @ a2)
    heads = lambda z: z.reshape(bsz, t, B_HEADS, HEAD_DIM).astype(jnp.float32)
    kk = heads(k * k_k)
    kk = kk / jnp.maximum(jnp.sqrt(jnp.sum(kk * kk, axis=-1, keepdims=True)), 1e-12)
    k = k * (1 + (a - 1) * k_a)
    rh, kh, vh, ah, wh = heads(r), heads(k), heads(v), heads(a), heads(decay)
    a_vec = -kk
    b_vec = kk * ah

    def step(state, inp):
        r_t, w_t, k_t, v_t, a_t, b_t = inp
        sa = jnp.einsum('bhvk,bhk->bhv', state, a_t)
        state = state * w_t[:, :, None, :] + sa[..., None] * b_t[:, :, None, :] + v_t[..., None] * k_t[:, :, None, :]
        return state, jnp.einsum('bhvk,bhk->bhv', state, r_t)

    seq_in = tuple(jnp.moveaxis(z, 1, 0) for z in (rh, wh, kh, vh, a_vec, b_vec))
    s0 = jnp.zeros((bsz, B_HEADS, HEAD_DIM, HEAD_DIM), jnp.float32)
    _, y = lax.scan(step, s0, seq_in)
    y = jnp.moveaxis(y, 0, 1)
    y = layer_norm(y, lnx_g.reshape(B_HEADS, HEAD_DIM), lnx_b.reshape(B_HEADS, HEAD_DIM), eps=GN_EPS)
    y = y + jnp.sum(rh * kh * r_k, axis=-1, keepdims=True) * vh
    return y.reshape(bsz, t, B_WIDTH).astype(p.dtype), v_raw


def chunk_band_attention(q, k, v, rel_table):
    bsz, t, _ = q.shape
    nc = t // CHUNK
    pad = LEFT_CHUNKS * CHUNK
    q = q.reshape(bsz, t, C_HEADS, HEAD_DIM)
    k_pad = jnp.pad(k.reshape(bsz, t, C_HEADS, HEAD_DIM), ((0, 0), (pad, 0), (0, 0), (0, 0)))
    v_pad = jnp.pad(v.reshape(bsz, t, C_HEADS, HEAD_DIM), ((0, 0), (pad, 0), (0, 0), (0, 0)))
    dist = jnp.arange(CHUNK)[:, None] + pad - jnp.arange(BAND)[None, :]
    idx = jnp.clip(dist, -REL_CLIP, REL_CLIP) + REL_CLIP
    bias = rel_table[:, idx].astype(jnp.float32)
    scale = HEAD_DIM ** -0.5

    def one_chunk(n):
        start = n * CHUNK
        qc = lax.dynamic_slice_in_dim(q, start, CHUNK, axis=1)
        kc = lax.dynamic_slice_in_dim(k_pad, start, BAND, axis=1)
        vc = lax.dynamic_slice_in_dim(v_pad, start, BAND, axis=1)
        s = jnp.einsum('bqhd,bkhd->bhqk', qc, kc).astype(jnp.float32) * scale + bias
        valid = (start - pad + jnp.arange(BAND)) >= 0
        s = jnp.where(valid[None, None, None, :], s, -jnp.inf)
        pr = jax.nn.softmax(s, axis=-1).astype(vc.dtype)
        return jnp.einsum('bhqk,bkhd->bqhd', pr, vc)

    out = lax.map(one_chunk, jnp.arange(nc))
    return jnp.moveaxis(out, 0, 1).reshape(bsz, t, C_WIDTH)


def hybrid_layer(x, c, v_first, w_ada, b_ada, w_in, sgu_ln_g, sgu_ln_b, w_spatial, b_spatial, mu_shift, w_decay0, w_decay2, a0, a2, k_k, k_a, r_k, lnx_g, lnx_b, vres, rel_bias, w_out, ln_g, ln_b):
    mod = jax.nn.silu(c) @ w_ada + b_ada
    shift, scale, gate = jnp.split(mod, 3, axis=-1)
    h = x * (1 + scale[:, None, :]) + shift[:, None, :]
    proj = h @ w_in
    pa, pb, pc = jnp.split(proj, [A_COLS, A_COLS + B_COLS], axis=-1)
    u_a, v_a, g_a = jnp.split(pa, 3, axis=-1)
    y_a = jax.nn.silu(g_a) * spatial_gating(u_a, v_a, sgu_ln_g, sgu_ln_b, w_spatial, b_spatial)
    y_b, v_raw = rwkv7_time_mix(pb[..., :B_SHIFT_COLS], mu_shift, w_decay0, w_decay2, a0, a2, k_k, k_a, r_k, lnx_g, lnx_b, v_first, vres)
    y_b = jax.nn.silu(pb[..., B_SHIFT_COLS:]) * y_b
    q_c, k_c, v_c, g_c = jnp.split(pc, 4, axis=-1)
    y_c = jax.nn.silu(g_c) * chunk_band_attention(q_c, k_c, v_c, rel_bias)
    y = jnp.concatenate([y_a, y_b, y_c], axis=-1) @ w_out
    x = layer_norm(DEEPNORM_ALPHA * x + (1 + gate[:, None, :]) * y, ln_g, ln_b)
    return x, v_raw


def setup_inputs(seed: int = 0) -> dict:
    key = jax.random.key(seed)
    ks = jax.random.split(key, 32)
    n = lambda i, shape: jax.random.normal(ks[i], shape, jnp.float32)
    L = DEPTH
    return {
        'x': n(0, (BATCH, SEQ, D_MODEL)),
        'c': n(1, (BATCH, D_MODEL)),
        'w_ada': n(2, (L, D_MODEL, 3 * D_MODEL)) * (0.1 * D_MODEL ** -0.5),
        'b_ada': n(3, (L, 3 * D_MODEL)) * 0.01,
        'w_in': n(4, (L, D_MODEL, PROJ_COLS)) * D_MODEL ** -0.5,
        'sgu_ln_g': 1.0 + 0.02 * n(5, (L, A_WIDTH)),
        'sgu_ln_b': 0.02 * n(6, (L, A_WIDTH)),
        'w_spatial': n(7, (L, A_GROUPS, SGU_BLOCK, SGU_BLOCK)) * SGU_BLOCK ** -0.5,
        'b_spatial': 1.0 + 0.02 * n(8, (L, A_GROUPS, SGU_BLOCK)),
        'mu_shift': jax.random.uniform(ks[9], (L, B_SHIFT_COLS), jnp.float32),
        'w_decay0': jax.random.uniform(ks[10], (L, B_WIDTH), jnp.float32, -6.0, -1.0),
        'w_decay2': n(11, (L, DECAY_LORA, B_WIDTH)) * 0.1,
        'a0': 0.1 * n(12, (L, B_WIDTH)),
        'a2': n(13, (L, AAA_LORA, B_WIDTH)) * (0.5 * AAA_LORA ** -0.5),
        'k_k': 0.85 + 0.02 * n(14, (L, B_WIDTH)),
        'k_a': 1.0 + 0.02 * n(15, (L, B_WIDTH)),
        'r_k': 0.1 * n(16, (L, B_HEADS, HEAD_DIM)),
        'lnx_g': 1.0 + 0.02 * n(17, (L, B_WIDTH)),
        'lnx_b': 0.02 * n(18, (L, B_WIDTH)),
        'v0': 1.0 + 0.1 * n(19, (L - 1, B_WIDTH)),
        'v1': n(20, (L - 1, B_WIDTH, MV_LORA)) * B_WIDTH ** -0.5,
        'v2': n(21, (L - 1, MV_LORA, B_WIDTH)) * (0.1 * MV_LORA ** -0.5),
        'rel_bias': 0.1 * n(22, (L, C_HEADS, 2 * REL_CLIP + 1)),
        'w_out': n(23, (L, MIX_WIDTH, D_MODEL)) * (DEEPNORM_BETA * MIX_WIDTH ** -0.5),
        'ln_g': 1.0 + 0.02 * n(24, (L, D_MODEL)),
        'ln_b': 0.02 * n(25, (L, D_MODEL)),
    }


def reference(x, c, w_ada, b_ada, w_in, sgu_ln_g, sgu_ln_b, w_spatial, b_spatial, mu_shift, w_decay0, w_decay2, a0, a2, k_k, k_a, r_k, lnx_g, lnx_b, v0, v1, v2, rel_bias, w_out, ln_g, ln_b):
    v_first = None
    for i in range(DEPTH):
        vres = (v0[i - 1], v1[i - 1], v2[i - 1]) if i > 0 else None
        x, v_raw = hybrid_layer(x, c, v_first, w_ada[i], b_ada[i], w_in[i], sgu_ln_g[i], sgu_ln_b[i], w_spatial[i], b_spatial[i], mu_shift[i], w_decay0[i], w_decay2[i], a0[i], a2[i], k_k[i], k_a[i], r_k[i], lnx_g[i], lnx_b[i], vres, rel_bias[i], w_out[i], ln_g[i], ln_b[i])
        if i == 0:
            v_first = v_raw
    return x
```

```python
import numpy as np
import ml_dtypes
import concourse.bass as bass
import concourse.mybir as mybir
from concourse.bass_utils import run_bass_kernel_spmd

F32 = mybir.dt.float32
BF16 = mybir.dt.bfloat16
AF = mybir.ActivationFunctionType
ALU = mybir.AluOpType
AX = mybir.AxisListType

D = 1024
PROJ = 3968
ALPHA = 4.0 ** 0.25
LN_EPS = 1e-5
GN_EPS = 64e-5
EPOCH = 12000
NPV = 52
R_LNG, R_LNB, R_SG, R_SB, R_XG, R_XB, NROW = 0, 1024, 2048, 2304, 2560, 2944, 3328
F_ID, F_ONE, F_SEL, NCF = 0, 128, 256, 258
B_ID, B_M4, B_P0, B_TRIL, B_ME, B_SCAN, B_SEL, B_NEG, NCB = 0, 128, 640, 768, 896, 1152, 1536, 1540, 1796


class Buf:
    def __init__(self, fw, name, handle, dma=False):
        self.name = name
        self.h = handle.ap() if type(handle).__name__.endswith("TensorHandle") else handle
        self.last_write = None
        self.readers = []
        self.dma_sem = fw.nc.alloc_semaphore("d_" + name) if dma else None
        self.dma_cnt = 0
        self.ready = 0.0
        self.rdone = 0.0

    def __getitem__(self, idx):
        return self.h[idx]


class EngState:
    def __init__(self, fw, name, obj):
        self.name = name
        self.obj = obj
        self.sem = fw.nc.alloc_semaphore(f"e_{name}_0")
        self.nsem = 1
        self.count = 0
        self.known = {}
        self.n_instr = 0
        self.n_wait = 0
        self.free = 0.0


class Fw:
    def __init__(self, nc):
        self.nc = nc
        self.eng = {}
        for name, obj in (("pe", nc.tensor), ("act", nc.scalar), ("dve", nc.vector),
                          ("pool", nc.gpsimd), ("sp", nc.sync)):
            self.eng[name] = EngState(self, name, obj)
        self.dma_bufs = []
        self.sb_bytes = 0
        self.psb_i = 0
        self.psbanks = []
        self.last_end = 0.0

    def sb(self, name, shape, dtype, dma=False):
        h = self.nc.alloc_sbuf_tensor("s_" + name, list(shape), dtype)
        n = 1
        for s in shape[1:]:
            n *= s
        self.sb_bytes += n * (4 if dtype == F32 else 2)
        b = Buf(self, name, h, dma)
        if dma:
            self.dma_bufs.append(b)
        return b

    def carve(self, name, ap, dma=False):
        b = Buf(self, name, ap, dma)
        if dma:
            self.dma_bufs.append(b)
        return b

    def dram(self, name, shape, dtype, kind="Internal"):
        h = self.nc.dram_tensor(name, list(shape), dtype, kind=kind)
        return Buf(self, name, h)

    def make_psum(self):
        for i in range(8):
            h = self.nc.alloc_psum_tensor(f"psb{i}", [128, 512], F32)
            self.psbanks.append(Buf(self, f"psb{i}", h))

    def psb(self):
        b = self.psbanks[self.psb_i % 6]
        self.psb_i += 1
        return b

    def _waits(self, e, reads, writes):
        need = {}

        def add(m):
            if m is None:
                return
            s, v = m
            k = s.num
            if k not in need or need[k][1] < v:
                need[k] = (s, v)

        for r in reads:
            add(r.last_write)
        for w in writes:
            add(w.last_write)
            for m in w.readers:
                add(m)
        for k, (s, v) in need.items():
            if e.name == "pe" and k == e.sem.num:
                continue
            if e.known.get(k, 0) >= v:
                continue
            e.obj.wait_ge(s, v)
            e.known[k] = v
            e.n_wait += 1

    def _mark(self, marker, reads, writes):
        for w in writes:
            w.last_write = marker
            w.readers = []
        for r in reads:
            if any(r is w for w in writes):
                continue
            r.readers.append(marker)
            if len(r.readers) > 48:
                best = {}
                for s, v in r.readers:
                    if s.num not in best or best[s.num][1] < v:
                        best[s.num] = (s, v)
                r.readers = list(best.values())

    _COST = {"pe": (60.0, 0.65), "act": (220.0, 0.6), "dve": (70.0, 1.0), "pool": (300.0, 2.2), "sp": (2500.0, 0.0)}

    def _vt(self, e, reads, writes, n):
        a, b = self._COST[e.name]
        st = e.free
        for r in reads:
            if r.ready > st:
                st = r.ready
        for w in writes:
            if w.ready > st:
                st = w.ready
            if w.rdone > st:
                st = w.rdone
        end = st + a + b * n
        e.free = end
        for w in writes:
            w.ready = end + 500.0
        for r in reads:
            if end > r.rdone:
                r.rdone = end
        self.last_end = end

    def op(self, en, fn, reads=(), writes=(), n=128):
        e = self.eng[en]
        self._vt(e, reads, writes, n)
        if e.count >= EPOCH:
            e.sem = self.nc.alloc_semaphore(f"e_{en}_{e.nsem}")
            e.nsem += 1
            e.count = 0
        self._waits(e, reads, writes)
        ins = fn(e.obj)
        e.count += 1
        e.n_instr += 1
        ins.then_inc(e.sem, 1)
        self._mark((e.sem, e.count), reads, writes)
        return ins

    def dma(self, pairs, reads=(), writes=(), sbuf=None, q="sp"):
        e = self.eng[q]
        self._vt(e, reads, writes, 0)
        self._waits(e, reads, writes)
        for o, i in pairs:
            e.obj.dma_start(out=o, in_=i).then_inc(sbuf.dma_sem, 16)
            sbuf.dma_cnt += 16
            e.n_instr += 1
        self._mark((sbuf.dma_sem, sbuf.dma_cnt), reads, writes)

    def barrier(self):
        for e in self.eng.values():
            for e2 in self.eng.values():
                if e2 is e or e2.count == 0:
                    continue
                if e.known.get(e2.sem.num, 0) < e2.count:
                    e.obj.wait_ge(e2.sem, e2.count)
                    e.known[e2.sem.num] = e2.count
            for b in self.dma_bufs:
                if b.dma_cnt and e.known.get(b.dma_sem.num, 0) < b.dma_cnt:
                    e.obj.wait_ge(b.dma_sem, b.dma_cnt)
                    e.known[b.dma_sem.num] = b.dma_cnt

    def stats(self):
        return {k: (e.n_instr, e.n_wait, e.nsem) for k, e in self.eng.items()}


class _Stop(Exception):
    pass


STOP = None
STOPN = None
import os as _os
BPRIO = float(_os.environ.get('BPRIO', '3000'))
_NSTEP = [0]


def build(T, L, NS=1, dbg=None):
    try:
        return _build(T, L, NS, dbg)
    except _Stop as e:
        nc, fw = e.args
        fw.barrier()
        return nc, fw


def _build(T, L, NS=1, dbg=None):
    nc = bass.Bass("TRN2", target_bir_lowering=False)
    fw = Fw(nc)
    _NSTEP[0] = 0
    NTT = T // 128
    assert T % 128 == 0

    x_d = fw.dram("x", [T, D], F32, "ExternalInput")
    cv_d = fw.dram("cvec", [128, 8], F32, "ExternalInput")
    wada_d = fw.dram("wada", [L, D, 3 * D], F32, "ExternalInput")
    win_d = fw.dram("win", [L, D, PROJ], F32, "ExternalInput")
    wout_d = fw.dram("wout", [L, D, D], F32, "ExternalInput")
    pv_d = fw.dram("pv", [L, 128, NPV], F32, "ExternalInput")
    rows_d = fw.dram("rows", [L, 128, NROW], F32, "ExternalInput")
    bsp_d = fw.dram("bsp", [L, 1, 512], F32, "ExternalInput")
    bgate_d = fw.dram("bgate", [L, 1, D], F32, "ExternalInput")
    wsT_d = fw.dram("wsT", [L, 128, 512], F32, "ExternalInput")
    lora_d = fw.dram("lora", [L, 128, 384], F32, "ExternalInput")
    v1_d = fw.dram("v1r", [128, 96], F32, "ExternalInput")
    v2_d = fw.dram("v2r", [32, 384], F32, "ExternalInput")
    bg_d = fw.dram("biasg", [L, 128, 3840], F32, "ExternalInput")
    cst_d = fw.dram("cstf", [128, NCF], F32, "ExternalInput")
    cstb_d = fw.dram("cstb", [128, NCB], BF16, "ExternalInput")
    out_d = fw.dram("out", [T, D], F32, "ExternalOutput")
    x1_d = fw.dram("x1s", [T, D], F32) if L > 1 else None
    vf_d = fw.dram("vfs", [384, T], F32) if L > 1 else None

    fw.make_psum()

    cst = fw.sb("cstf_s", [128, NCF], F32, dma=True)
    cstb = fw.sb("cstb_s", [128, NCB], BF16, dma=True)
    win = fw.sb("win", [128, 8, PROJ], BF16)
    wout = fw.sb("woutb", [128, 8, D], BF16)
    rows = fw.sb("rows", [128, NROW], F32, dma=True)
    pv = fw.sb("pv", [128, NPV], F32, dma=True)
    eb = fw.sb("eb", [128, 5, 6, 128], BF16)
    wsT = fw.sb("wsTb", [128, 4, 128], BF16)
    lora = fw.sb("lorab", [128, 384], BF16)
    v1b = fw.sb("v1b", [128, 3, 32], BF16)
    v2b = fw.sb("v2b", [32, 384], BF16)
    bsp = fw.sb("bsp", [1, 512], F32, dma=True)
    cvec = fw.sb("cvec", [128, 8], F32, dma=True)
    mod = fw.sb("mod", [128, 16], F32)
    dp = fw.sb("dp", [128, 16], F32)
    mhalf = fw.sb("mhalf", [128, 8], F32)
    hT = fw.sb("hT", [128, 8, 128], BF16)
    xs = [fw.sb(f"xs{i}", [128, D], F32, dma=True) for i in range(2)]
    uT = [fw.sb(f"uT{i}", [128, 2, 128], BF16) for i in range(2)]
    gaT = [fw.sb(f"gaT{i}", [128, 2, 128], BF16) for i in range(2)]
    gbT = [fw.sb(f"gbT{i}", [128, 3, 128], BF16) for i in range(2)]
    gcT = [fw.sb(f"gcT{i}", [128, 3, 128], BF16) for i in range(2)]
    qT = [fw.sb(f"qT{i}", [128, 3, 128], BF16) for i in range(2)]
    mixT = [fw.sb(f"mixT{i}", [128, 8, 128], BF16) for i in range(2)]
    kT = fw.sb("kT", [128, 3, 768], BF16)
    vaug = fw.sb("vaug", [128, 6, 6, 68], BF16)
    pB = fw.sb("pB", [128, 10, 129], F32)
    sg1s = [fw.sb(f"sg1_{i}", [128, 256], F32) for i in range(2)]
    sg2 = fw.sb("sg2", [128, 256], F32)
    vnb = fw.sb("vnb", [128, 256], BF16)
    stA = fw.sb("stA", [128, 8], F32)
    stB = fw.sb("stB", [128, 24], F32)
    stB2 = fw.sb("stB2", [128, 24], F32)
    stC = fw.sb("stC", [128, 8], F32)
    stO = fw.sb("stO", [128, 16], F32)
    PTs = [fw.sb(f"PT{i}", [128, 2, 3, 128], BF16) for i in range(2)]
    att = fw.sb("att", [128, 384], BF16)
    Hs = fw.sb("Hs", [128, 3, 64], F32)
    Hb = fw.sb("Hb", [128, 3, 128], BF16)
    ATb = fw.sb("ATb", [128, 6, 4, 128], BF16)
    PQS = [fw.sb(f"PQS{h}", [128, 3, 128], BF16) for h in range(6)]
    HO = []
    for i in range(2):
        HO.append(dict(
            ARb=fw.sb(f"ARb{i}", [128, 3, 256], BF16), BKb=fw.sb(f"BKb{i}", [128, 3, 256], BF16),
            bhb=fw.sb(f"bhb{i}", [128, 3, 128], BF16), khb=fw.sb(f"khb{i}", [128, 3, 128], BF16),
            vbf=fw.sb(f"vbf{i}", [128, 3, 128], BF16), wc=fw.sb(f"wc{i}", [128, 3, 2], F32),
            rks=fw.sb(f"rks{i}", [128, 8], F32)))
    arena = nc.alloc_sbuf_tensor("arena", [128, 8320], F32)
    fw.sb_bytes += 8320 * 4
    aap = arena.ap()
    stage = [fw.carve(f"stage{i}", aap[:, i * 4096:(i + 1) * 4096], dma=True) for i in range(2)]
    _off = [0]

    def cv(name, nfree, dtype=F32, shape3=None, dma=False):
        n32 = nfree if dtype == F32 else (nfree + 1) // 2
        ap = aap[:, _off[0]:_off[0] + n32]
        _off[0] += n32
        assert _off[0] <= 8320, name
        if dtype == BF16:
            ap = ap.bitcast(BF16)
        if shape3:
            ap = ap.rearrange("p (a b) -> p a b", a=shape3[0])
        return fw.carve(name, ap, dma)

    xsB = cv("xsB", 1280, shape3=(10, 128))
    vfT = cv("vfT", 384, shape3=(3, 128), dma=True)
    tA = cv("tA", 384, shape3=(3, 128))
    tK = cv("tK", 384, shape3=(3, 128))
    tM = cv("tM", 384, shape3=(3, 128))
    tBv = cv("tBv", 384, shape3=(3, 128))
    tL = cv("tL", 384, shape3=(3, 128))
    tC = cv("tC", 384, shape3=(3, 128))
    tEi = cv("tEi", 384, shape3=(3, 128))
    tEn = cv("tEn", 384, shape3=(3, 128))
    tEe = cv("tEe", 384, shape3=(3, 128))
    t1 = cv("t1", 384, shape3=(3, 128))
    t2 = cv("t2", 384, shape3=(3, 128))
    ytm = cv("ytm", 384, shape3=(6, 64))
    t3 = cv("t3", 384, shape3=(3, 128))
    twb = cv("twb", 128, BF16)
    Vtm = cv("Vtm", 384, BF16)
    Btm = cv("Btm", 384, BF16)
    Ktm = cv("Ktm", 384, BF16)
    Xb = cv("Xb", 384, BF16, shape3=(6, 64))
    Ub = cv("Ub", 384, BF16, shape3=(6, 64))
    ybf = cv("ybf", 384, BF16)
    vv1 = cv("vv1", 128, BF16)
    rkb = cv("rkb", 384, BF16, shape3=(3, 128))

    def mm(out, lhsT, rhs, start, stop, R, W, sgc=False):
        fw.op("pe", lambda e: e.matmul(out, lhsT=lhsT, rhs=rhs, start=start, stop=stop, skip_group_check=sgc), R, W,
              n=rhs.free_size())

    def tr(out, in_, ident, R, W):
        fw.op("pe", lambda e: e.transpose(out, in_, ident), R, W, n=128)

    def act(out, in_, func, R, W, bias=None, scale=None):
        kw = {}
        if bias is not None:
            kw["bias"] = bias
        if scale is not None:
            kw["scale"] = scale
        fw.op("act", lambda e: e.activation(out=out, in_=in_, func=func, **kw), R, W, n=out.free_size())

    def tt(en, out, in0, in1, op, R, W):
        fw.op(en, lambda e: e.tensor_tensor(out=out, in0=in0, in1=in1, op=op), R, W, n=out.free_size())

    def ts(en, out, in0, s1, s2, op0, op1, R, W):
        if s2 is None:
            fw.op(en, lambda e: e.tensor_scalar(out=out, in0=in0, scalar1=s1, scalar2=None, op0=op0), R, W, n=out.free_size())
        else:
            fw.op(en, lambda e: e.tensor_scalar(out=out, in0=in0, scalar1=s1, scalar2=s2, op0=op0, op1=op1), R, W, n=out.free_size())

    def stt(out, in0, scalar, in1, op0, op1, R, W):
        fw.op("dve", lambda e: e.scalar_tensor_tensor(out=out, in0=in0, scalar=scalar, in1=in1, op0=op0, op1=op1), R, W, n=out.free_size())

    def cp(en, out, in_, R, W):
        if en == "act":
            fw.op("act", lambda e: e.copy(out=out, in_=in_), R, W, n=out.free_size())
        else:
            fw.op(en, lambda e: e.tensor_copy(out=out, in_=in_), R, W, n=out.free_size())

    def rsqrt_pool(ap, buf):
        n = ap.shape[1]
        fw.op("pool", lambda e: e.tensor_tensor(out=ap, in0=ap, in1=mhalf[:, 0:n], op=ALU.pow), [buf, mhalf], [buf])

    def chk(stage):
        if STOP == stage:
            raise _Stop(nc, fw)

    fw.dma([(cst[:, :], cst_d[:, :])], reads=[], writes=[cst], sbuf=cst)
    fw.dma([(cstb[:, :], cstb_d[:, :])], reads=[], writes=[cstb], sbuf=cstb)
    identf = cst[:, F_ID:F_ID + 128]
    onesf = cst[:, F_ONE:F_ONE + 128]
    sel2f = cst[:, F_SEL:F_SEL + 2]
    identb = cstb[:, B_ID:B_ID + 128]
    M4 = cstb[:, B_M4:B_M4 + 512]
    MP0 = cstb[:, B_P0:B_P0 + 128]
    tril = cstb[:, B_TRIL:B_TRIL + 128]
    maskE = cstb[:, B_ME:B_ME + 256]
    scanm = cstb[:, B_SCAN:B_SCAN + 384]
    sel2b = cstb[:, B_SEL:B_SEL + 2]
    negE = cstb[:, B_NEG:B_NEG + 256]
    fw.dma([(cvec[:, :], cv_d[:, :])], reads=[], writes=[cvec], sbuf=cvec)
    fw.op("pool", lambda e: e.memset(vaug[:, :, :, 64:68], 1.0), [], [vaug])
    fw.op("pool", lambda e: e.memset(mhalf[:, :], -0.5), [], [mhalf])
    C1 = -0.5 * float(np.exp(-0.5))

    for l in range(L):
        src_d = x_d if l == 0 else x1_d
        dst_d = out_d if l == L - 1 else x1_d
        fw.barrier()
        fw.dma([(pv[:, :], pv_d[l])], writes=[pv], sbuf=pv)
        fw.dma([(rows[:, :], rows_d[l])], writes=[rows], sbuf=rows)
        fw.dma([(bsp[:, :], bsp_d[l])], writes=[bsp], sbuf=bsp)
        ts("dve", dp[:, 0:6], pv[:, 10:16], 0.5, None, ALU.mult, None, [pv], [dp])
        ts("dve", dp[:, 6:9], pv[:, 25:28], 0.5, None, ALU.mult, None, [pv], [dp])
        ts("dve", dp[:, 9:12], pv[:, 19:22], -1.0, 1.0, ALU.mult, ALU.add, [pv], [dp])
        act(stB[:, 8:16], cvec[:, :], AF.Tanh, [cvec], [stB], scale=0.5)
        stt(stB[:, 0:8], stB[:, 8:16], 1.0, cvec[:, :], ALU.add, ALU.mult, [stB, cvec], [stB])
        ts("dve", stB[:, 0:8], stB[:, 0:8], 0.5, None, ALU.mult, None, [stB], [stB])
        modps = fw.psb()
        growps = [fw.psb(), fw.psb()]
        for ch in range(6):
            sg = stage[ch % 2]
            sgv = sg.h.rearrange("p (k n) -> p k n", k=8)
            fw.dma([(sgv[:, k, :], wada_d[l, k * 128:(k + 1) * 128, ch * 512:(ch + 1) * 512]) for k in range(8)],
                   writes=[sg], sbuf=sg)
            if ch < 4:
                for jj in range(4):
                    j = ch * 4 + jj
                    for k in range(8):
                        mm(modps[:, j:j + 1], sgv[:, k, jj * 128:(jj + 1) * 128], stB[:, k:k + 1],
                           k == 0, k == 7, [sg, stB], [modps])
            else:
                gp = growps[ch - 4]
                for k in range(8):
                    mm(gp[0:1, :], stB[:, k:k + 1], sgv[:, k, :], k == 0, k == 7, [sg, stB], [gp])
        tt("dve", mod[:, :], modps[:, 0:16], pv[:, 28:44], ALU.add, [modps, pv], [mod])
        ts("dve", mod[:, 8:16], mod[:, 8:16], 1.0, None, ALU.add, None, [mod], [mod])
        g1r = stage[1]
        fw.dma([(g1r[0:1, 0:1024], bgate_d[l])], writes=[g1r], sbuf=g1r)
        for i in range(2):
            tt("dve", g1r[0:1, i * 512:(i + 1) * 512], g1r[0:1, i * 512:(i + 1) * 512], growps[i][0:1, :], ALU.add,
               [g1r, growps[i]], [g1r])
        ts("dve", g1r[0:1, 0:1024], g1r[0:1, 0:1024], 1.0, 0.5, ALU.add, ALU.mult, [g1r], [g1r])
        g1bc = [fw.psb(), fw.psb()]
        for i in range(2):
            mm(g1bc[i][:, :], onesf[0:1, :], g1r[0:1, i * 512:(i + 1) * 512], True, True, [cst, g1r], [g1bc[i]])
        for i in range(2):
            sg = stage[i]
            sgv = sg.h.rearrange("p (k n) -> p k n", k=8)
            fw.dma([(sgv[:, k, :], wout_d[l, k * 128:(k + 1) * 128, i * 512:(i + 1) * 512]) for k in range(8)],
                   writes=[sg], sbuf=sg)
            for k in range(8):
                tt("dve", wout[:, k, i * 512:(i + 1) * 512], sgv[:, k, :], g1bc[i][:, :], ALU.mult,
                   [sg, g1bc[i]], [wout])
        for ch in range(8):
            sg = stage[ch % 2]
            sgv = sg.h[:, 0:3968].rearrange("p (k n) -> p k n", k=8)
            fw.dma([(sgv[:, k, :], win_d[l, k * 128:(k + 1) * 128, ch * 496:(ch + 1) * 496]) for k in range(8)],
                   writes=[sg], sbuf=sg)
            en = "act" if ch % 2 == 0 else "dve"
            cp(en, win[:, :, ch * 496:(ch + 1) * 496], sgv[:, :, :], [sg], [win])
        sg = stage[0]
        fw.dma([(sg[:, 0:512], wsT_d[l])], writes=[sg], sbuf=sg)
        tt("dve", wsT[:, :, :], sg.h[:, 0:512].rearrange("p (g n) -> p g n", g=4),
           tril.unsqueeze(1).to_broadcast([128, 4, 128]), ALU.mult, [sg, cstb], [wsT])
        fw.dma([(sg[:, 512:896], lora_d[l])], writes=[sg], sbuf=sg)
        cp("dve", lora[:, :], sg[:, 512:896], [sg], [lora])
        if l > 0:
            fw.dma([(sg[:, 896:992], v1_d[:, :]), (sg[0:32, 1024:1408], v2_d[:, :])], writes=[sg], sbuf=sg)
            cp("dve", v1b[:, :, :], sg.h[:, 896:992].rearrange("p (a b) -> p a b", a=3), [sg], [v1b])
            cp("dve", v2b[:, :], sg[0:32, 1024:1408], [sg], [v2b])
        sg = stage[1]
        fw.dma([(sg[:, 0:3840], bg_d[l])], writes=[sg], sbuf=sg)
        for jj in range(5):
            src = sg.h[:, jj * 768:(jj + 1) * 768].rearrange("p (h q) -> p h q", h=6)
            if jj in (0, 4):
                mi = 0 if jj == 0 else 1
                tt("dve", src, src, maskE[:, mi * 128:(mi + 1) * 128].unsqueeze(1).to_broadcast([128, 6, 128]), ALU.mult,
                   [sg, cstb], [sg])
                tt("dve", eb[:, jj, :, :], src, negE[:, mi * 128:(mi + 1) * 128].unsqueeze(1).to_broadcast([128, 6, 128]),
                   ALU.add, [sg, cstb], [eb])
            else:
                cp("dve", eb[:, jj, :, :], src, [sg], [eb])
        fw.barrier()
        fw.op("dve", lambda e: e.memset(Hs[:, :, :], 0.0), [], [Hs])
        fw.op("dve", lambda e: e.memset(Hb[:, :, :], 0.0), [], [Hb])
        fw.op("pool", lambda e: e.memset(pB[:, :, 0:1], 0.0), [], [pB])

        def gen_proj(m):
            t0 = m * 128
            X = xs[m % 2]
            P = m % 2
            slot = m % 6
            fw.dma([(X[:, :], src_d[t0:t0 + 128, :])], reads=[src_d], writes=[X], sbuf=X)
            yield
            for k2 in range(2):
                pt = fw.psb()
                for kk in range(4):
                    k = k2 * 4 + kk
                    tr(pt[:, kk * 128:(kk + 1) * 128], X[:, k * 128:(k + 1) * 128], identf, [X, cst], [pt])
                for kk in range(4):
                    k = k2 * 4 + kk
                    act(hT[:, k, :], pt[:, kk * 128:(kk + 1) * 128], AF.Identity, [pt, mod], [hT],
                        bias=mod[:, k:k + 1], scale=mod[:, 8 + k:9 + k])
                yield

            def proj_fm(c0, n):
                pp = fw.psb()
                for j in range(n):
                    for k in range(8):
                        mm(pp[:, j * 128:(j + 1) * 128], win[:, k, c0 + j * 128:c0 + (j + 1) * 128], hT[:, k, :],
                           k == 0, k == 7, [win, hT], [pp])
                return pp

            def gate(pp, n, dst):
                act(dst[:, :, :], pp.h[:, 0:n * 128].rearrange("p (j t) -> p j t", j=n), AF.Tanh, [pp], [dst], scale=0.5)
                stt(dst[:, :, :], dst[:, :, :], 1.0, pp.h[:, 0:n * 128].rearrange("p (j t) -> p j t", j=n),
                    ALU.add, ALU.mult, [dst, pp], [dst])

            for c0, j0, n in ((768, 0, 4), (1280, 4, 4), (1792, 8, 2)):
                pp = proj_fm(c0, n)
                cp("act", pB[:, j0:j0 + n, 1:129], pp.h[:, 0:n * 128].rearrange("p (j t) -> p j t", j=n), [pp], [pB])
                yield
            pp = proj_fm(2048, 3)
            gate(pp, 3, gbT[P])
            yield
            pp = proj_fm(2432, 3)
            act(qT[P][:, :, :], pp.h[:, 0:384].rearrange("p (j t) -> p j t", j=3), AF.Copy, [pp], [qT[P]], scale=0.125)
            yield
            pp = proj_fm(2816, 3)
            cp("act", kT[:, :, slot * 128:(slot + 1) * 128], pp.h[:, 0:384].rearrange("p (j t) -> p j t", j=3), [pp], [kT])
            yield
            pp = fw.psb()
            for k in range(8):
                mm(pp[:, 0:384], hT[:, k, :], win[:, k, 3200:3584], k == 0, k == 7, [hT, win], [pp])
            cp("act", vaug[:, slot, :, 0:64], pp.h[:, 0:384].rearrange("p (h d) -> p h d", h=6), [pp], [vaug])
            yield
            pp = proj_fm(3584, 3)
            gate(pp, 3, gcT[P])
            yield
            pp = proj_fm(0, 2)
            cp("act", uT[P][:, :, :], pp.h[:, 0:256].rearrange("p (j t) -> p j t", j=2), [pp], [uT[P]])
            yield
            pp = proj_fm(512, 2)
            gate(pp, 2, gaT[P])
            yield
            pa = fw.psb()
            for k in range(8):
                mm(pa[:, 0:256], hT[:, k, :], win[:, k, 256:512], k == 0, k == 7, [hT, win], [pa])
            cp("act", sg1s[P][:, :], pa[:, 0:256], [pa], [sg1s[P]])
            yield

        def gen_A(m):
            P = m % 2
            sg1 = sg1s[P]
            av3 = sg1.h.rearrange("p (g c) -> p g c", g=4)
            s23 = sg2.h.rearrange("p (g c) -> p g c", g=4)
            fw.op("dve", lambda e: e.tensor_reduce(out=stA[:, 0:4], in_=av3, axis=AX.X, op=ALU.add), [sg1], [stA])
            ts("dve", stA[:, 0:4], stA[:, 0:4], 1.0 / 64, None, ALU.mult, None, [stA], [stA])
            tt("dve", av3, av3, stA[:, 0:4].unsqueeze(2).to_broadcast([128, 4, 64]), ALU.subtract, [sg1, stA], [sg1])
            yield
            act(sg2[:, :], sg1[:, :], AF.Square, [sg1], [sg2])
            fw.op("dve", lambda e: e.tensor_reduce(out=stA[:, 4:8], in_=s23, axis=AX.X, op=ALU.add), [sg2], [stA])
            ts("dve", stA[:, 4:8], stA[:, 4:8], 1.0 / 64, LN_EPS, ALU.mult, ALU.add, [stA], [stA])
            rsqrt_pool(stA[:, 4:8], stA)
            yield
            tt("dve", av3, av3, stA[:, 4:8].unsqueeze(2).to_broadcast([128, 4, 64]), ALU.mult, [sg1, stA], [sg1])
            tt("pool", sg1[:, :], sg1[:, :], rows[:, R_SG:R_SG + 256], ALU.mult, [sg1, rows], [sg1])
            tt("pool", vnb[:, :], sg1[:, :], rows[:, R_SB:R_SB + 256], ALU.add, [sg1, rows], [vnb])
            yield
            psa = fw.psb()
            for g in range(4):
                o = psa[(g % 2) * 64:(g % 2) * 64 + 64, (g // 2) * 128:(g // 2) * 128 + 128]
                mm(o, vnb[:, g * 64:(g + 1) * 64], wsT[:, g, :], True, False, [vnb, wsT], [psa])
                mm(o, onesf[0:1, 0:64], bsp[0:1, g * 128:(g + 1) * 128], False, True, [cst, bsp], [psa])
            s2v = sg2.h.rearrange("p (j t) -> p j t", j=2)
            tt("dve", s2v, psa.h[:, 0:256].rearrange("p (j t) -> p j t", j=2), uT[P][:, :, :], ALU.mult, [psa, uT[P]], [sg2])
            tt("pool", mixT[P][:, 0:2, :], s2v, gaT[P][:, :, :], ALU.mult, [sg2, gaT[P]], [mixT[P]])
            yield

        def gen_C(m):
            P = m % 2
            jlo = max(0, m - 4)
            po_ = fw.psbanks[6]
            o3 = po_.h[:, 0:408].rearrange("p (h d) -> p h d", h=6)
            first = True
            ebv = eb.h.rearrange("p j h q -> p j (h q)")
            for j in range(jlo, m + 1):
                jj = j - (m - 4)
                ks = (j % 6) * 128
                PT = PTs[j % 2]
                pS = [fw.psb(), fw.psb()]
                for par in range(2):
                    mm(pS[par][:, 0:384], identb, ebv[:, jj, par * 384:(par + 1) * 384], True, False, [cstb, eb], [pS[par]])
                for h3 in range(3):
                    for par in range(2):
                        po = par * 64
                        mm(pS[par][:, h3 * 128:(h3 + 1) * 128], kT[po:po + 64, h3, ks:ks + 128],
                           qT[P][po:po + 64, h3, :], False, h3 == 2, [kT, qT[P]], [pS[par]])
                for par in range(2):
                    act(PT[:, par, :, :], pS[par].h[:, 0:384].rearrange("p (h q) -> p h q", h=3), AF.Exp, [pS[par]], [PT])
                for h in range(6):
                    par, h3 = h % 2, h // 2
                    mm(o3[:, h, :], PT[:, par, h3, :], vaug[:, j % 6, h, :], first, (j == m and h == 5), [PT, vaug], [po_], sgc=True)
                    first = False
                yield
            fw.op("dve", lambda e: e.reciprocal(out=stC[:, 0:6], in_=o3[:, :, 64]), [po_], [stC])
            tt("dve", att.h.rearrange("p (h d) -> p h d", h=6), o3[:, :, 0:64],
               stC[:, 0:6].unsqueeze(2).to_broadcast([128, 6, 64]), ALU.mult, [po_, stC], [att])
            yield
            pt = fw.psb()
            ptb = pt.h.bitcast(BF16)
            for jp in range(3):
                tr(ptb[:, jp * 128:(jp + 1) * 128], att[:, jp * 128:(jp + 1) * 128], identb, [att, cstb], [pt])
            tt("dve", mixT[P][:, 5:8, :], ptb[:, 0:384].rearrange("p (j t) -> p j t", j=3), gcT[P][:, :, :],
               ALU.mult, [pt, gcT[P]], [mixT[P]])
            yield

        def gen_B1(m):
            P = m % 2
            t0 = m * 128
            ho = HO[P]
            ARb, BKb, bhb, khb, vbf, rks = ho['ARb'], ho['BKb'], ho['bhb'], ho['khb'], ho['vbf'], ho['rks']
            W3 = pB[:, :, 1:129]
            Wp3 = pB[:, :, 0:128]
            tt("dve", xsB[:, :, :], Wp3, W3, ALU.subtract, [pB], [xsB])
            tt("dve", xsB[:, :, :], xsB[:, :, :], pv[:, 0:10].unsqueeze(2).to_broadcast([128, 10, 128]), ALU.mult,
               [xsB, pv], [xsB])
            tt("pool", xsB[:, :, :], xsB[:, :, :], W3, ALU.add, [xsB, pB], [xsB])
            cp("pool", pB[:, :, 0:1], pB[:, :, 128:129], [pB], [pB])
            yield
            r3 = xsB[:, 0:3, :]
            k3 = xsB[:, 3:6, :]
            v3 = xsB[:, 6:9, :]
            if L > 1 and l == 0:
                fw.dma([(vf_d[jp * 128:(jp + 1) * 128, t0:t0 + 128], xsB[:, 6 + jp, :])
                        for jp in range(3)], reads=[xsB], writes=[vf_d], sbuf=vfT)
            if l > 0:
                fw.dma([(vfT[:, jp, :], vf_d[jp * 128:(jp + 1) * 128, t0:t0 + 128])
                        for jp in range(3)], reads=[vf_d], writes=[vfT], sbuf=vfT)
            act(twb[0:64, :], xsB[0:64, 9, :], AF.Tanh, [xsB], [twb])
            cp("act", twb[64:128, :], xsB[64:128, 9, :], [xsB], [twb])
            pz = fw.psb()
            pa2 = fw.psb()
            for jp in range(3):
                mm(pz[:, jp * 128:(jp + 1) * 128], lora[0:64, jp * 128:(jp + 1) * 128], twb[0:64, :], True, True,
                   [lora, twb], [pz])
                mm(pa2[:, jp * 128:(jp + 1) * 128], lora[64:128, jp * 128:(jp + 1) * 128], twb[64:128, :], True, True,
                   [lora, twb], [pa2])
            for jp in range(3):
                act(tL[:, jp, :], pz[:, jp * 128:(jp + 1) * 128], AF.Tanh, [pz, dp], [tL], bias=dp[:, jp:jp + 1], scale=0.5)
                act(tA[:, jp, :], pa2[:, jp * 128:(jp + 1) * 128], AF.Tanh, [pa2, dp], [tA], bias=dp[:, 3 + jp:4 + jp], scale=0.5)
            ts("dve", tL[:, :, :], tL[:, :, :], C1, C1, ALU.mult, ALU.add, [tL], [tL])
            ts("dve", tA[:, :, :], tA[:, :, :], 0.5, 0.5, ALU.mult, ALU.add, [tA], [tA])
            yield
            if l > 0:
                cp("act", vbf[:, :, :], v3, [xsB], [vbf])
                pv1 = fw.psb()
                for jp in range(3):
                    mm(pv1[0:32, 0:128], v1b[:, jp, :], vbf[:, jp, :], jp == 0, jp == 2, [v1b, vbf], [pv1])
                cp("act", vv1[0:32, :], pv1[0:32, 0:128], [pv1], [vv1])
                pv2 = fw.psb()
                for jp in range(3):
                    mm(pv2[:, jp * 128:(jp + 1) * 128], v2b[0:32, jp * 128:(jp + 1) * 128], vv1[0:32, :], True, True,
                       [v2b, vv1], [pv2])
                for jp in range(3):
                    act(t1[:, jp, :], pv2[:, jp * 128:(jp + 1) * 128], AF.Tanh, [pv2, dp], [t1],
                        bias=dp[:, 6 + jp:7 + jp], scale=0.5)
                tt("dve", t2[:, :, :], vfT[:, :, :], v3, ALU.subtract, [vfT, xsB], [t2])
                stt(t2[:, :, :], t1[:, :, :], 1.0, t2[:, :, :], ALU.add, ALU.mult, [t1, t2], [t2])
                stt(v3, t2[:, :, :], 0.5, v3, ALU.mult, ALU.add, [t2, xsB], [xsB])
                yield
            cp("act", vbf[:, :, :], v3, [xsB], [vbf])
            tt("dve", tK[:, :, :], k3, pv[:, 16:19].unsqueeze(2).to_broadcast([128, 3, 128]), ALU.mult, [xsB, pv], [tK])
            act(t1[:, :, :], tK[:, :, :], AF.Square, [tK], [t1])
            pss = fw.psb()
            for jp in range(3):
                mm(pss[:, jp * 2:jp * 2 + 2], t1[:, jp, :], sel2f, True, True, [t1, cst], [pss])
            ts("dve", stB[:, 0:6], pss[:, 0:6], 1e-24, None, ALU.max, None, [pss], [stB])
            rsqrt_pool(stB[:, 0:6], stB)
            yield
            t2y = t2.h.rearrange("p a b -> p (a b)").rearrange("p (h d) -> p h d", h=6)
            cp("dve", t2y, stB[:, 0:6].unsqueeze(2).to_broadcast([128, 6, 64]), [stB], [t2])
            prn = fw.psb()
            for jp in range(3):
                tr(prn[:, jp * 128:(jp + 1) * 128], t2.h.rearrange("p a b -> p (a b)")[:, jp * 128:(jp + 1) * 128], identf,
                   [t2, cst], [prn])
            tt("dve", tK[:, :, :], tK[:, :, :], prn.h[:, 0:384].rearrange("p (a b) -> p a b", a=3), ALU.mult, [tK, prn], [tK])
            yield
            for jp in range(3):
                ts("dve", tM[:, jp, :], tA[:, jp, :], pv[:, 19 + jp:20 + jp], dp[:, 9 + jp:10 + jp], ALU.mult, ALU.add, [tA, pv, dp], [tM])
            tt("dve", tM[:, :, :], tM[:, :, :], k3, ALU.mult, [tM, xsB], [tM])
            tt("dve", tBv[:, :, :], tK[:, :, :], tA[:, :, :], ALU.mult, [tK, tA], [tBv])
            yield
            tt("pool", t1[:, :, :], r3, tM[:, :, :], ALU.mult, [xsB, tM], [t1])
            tt("pool", rkb[:, :, :], t1[:, :, :], pv[:, 22:25].unsqueeze(2).to_broadcast([128, 3, 128]), ALU.mult, [t1, pv], [rkb])
            prk = fw.psb()
            for jp in range(3):
                mm(prk[:, jp * 2:jp * 2 + 2], rkb[:, jp, :], sel2b, True, True, [rkb, cstb], [prk])
            cp("act", rks[:, 0:6], prk[:, 0:6], [prk], [rks])
            yield
            tCf = tC.h.rearrange("p a b -> p (a b)")
            tLf = tL.h.rearrange("p a b -> p (a b)")
            fw.op("dve", lambda e: e.tensor_tensor_scan(out=tCf, data0=scanm, data1=tLf, initial=0.0,
                                                        op0=ALU.mult, op1=ALU.add), [cstb, tL], [tC])
            act(tEi[:, :, :], tC[:, :, :], AF.Exp, [tC], [tEi])
            act(tEn[:, :, :], tC[:, :, :], AF.Exp, [tC], [tEn], scale=-1.0)
            tt("dve", t1[:, :, :], tC[:, :, :], tL[:, :, :], ALU.subtract, [tC, tL], [t1])
            act(t1[:, :, :], t1[:, :, :], AF.Exp, [t1], [t1])
            yield
            ei4 = tEi.h.rearrange("p a (c t) -> p a c t", c=2)
            tt("dve", tEe.h.rearrange("p a (c t) -> p a c t", c=2), tEn.h.rearrange("p a (c t) -> p a c t", c=2),
               ei4[:, :, :, 63:64].to_broadcast([128, 3, 2, 64]), ALU.mult, [tEn, tEi], [tEe])
            AR4 = ARb.h.rearrange("p a (w t) -> p a w t", w=2)
            BK4 = BKb.h.rearrange("p a (w t) -> p a w t", w=2)
            stt(AR4[:, :, 0, :], tK[:, :, :], -1.0, t1[:, :, :], ALU.mult, ALU.mult, [tK, t1], [ARb])
            tt("pool", AR4[:, :, 1, :], r3, tEi[:, :, :], ALU.mult, [xsB, tEi], [ARb])
            tt("dve", BK4[:, :, 0, :], tBv[:, :, :], tEn[:, :, :], ALU.mult, [tBv, tEn], [BKb])
            tt("pool", BK4[:, :, 1, :], tM[:, :, :], tEn[:, :, :], ALU.mult, [tM, tEn], [BKb])
            yield
            tt("dve", bhb[:, :, :], tBv[:, :, :], tEe[:, :, :], ALU.mult, [tBv, tEe], [bhb])
            tt("pool", khb[:, :, :], tM[:, :, :], tEe[:, :, :], ALU.mult, [tM, tEe], [khb])
            cp("act", ho['wc'][:, :, :], tEi.h.rearrange("p a (c t) -> p a c t", c=2)[:, :, :, 63], [tEi], [ho['wc']])
            yield

        def gen_B2(m):
            P = m % 2
            ho = HO[P]
            ARb, BKb, bhb, khb, vbf, rks = ho['ARb'], ho['BKb'], ho['bhb'], ho['khb'], ho['vbf'], ho['rks']
            for h in range(6):
                jp, po = h // 2, (h % 2) * 64
                pA = fw.psb()
                mm(pA[:, 0:256], BKb[po:po + 64, jp, 0:128], ARb[po:po + 64, jp, :], True, True, [BKb, ARb], [pA])
                mm(pA[:, 256:512], BKb[po:po + 64, jp, 128:256], ARb[po:po + 64, jp, :], True, True, [BKb, ARb], [pA])
                tt("dve", ATb[:, h, 1:4, :], pA.h[:, 128:512].rearrange("p (w t) -> p w t", w=3),
                   M4[:, 128:512].rearrange("p (w t) -> p w t", w=3), ALU.mult, [pA, cstb], [ATb])
                tt("dve", PQS[h][:, 1, :], pA[:, 0:128], M4[:, 0:128], ALU.mult, [pA, cstb], [PQS[h]])
                pP = fw.psb()
                mm(pP[:, 0:128], ARb[po:po + 64, jp, 0:128], BKb[po:po + 64, jp, 0:128], True, True, [ARb, BKb], [pP])
                tt("dve", PQS[h][:, 0, :], pP[:, 0:128], MP0, ALU.mult, [pP, cstb], [PQS[h]])
                cp("act", PQS[h][:, 2, :], identb, [cstb], [PQS[h]])
                if h % 2:
                    yield
            yield
            for srcb, dstb in ((vbf, Vtm), (bhb, Btm), (khb, Ktm)):
                pt = fw.psb()
                ptb = pt.h.bitcast(BF16)
                for jp in range(3):
                    tr(ptb[:, jp * 128:(jp + 1) * 128], srcb[:, jp, :], identb, [srcb, cstb], [pt])
                cp("act", dstb[:, :], ptb[:, 0:384], [pt], [dstb])
            yield
            for i in range(6):
                for h in range(6):
                    B_ = PQS[h]
                    pp_ = fw.psb()
                    if i < 5:
                        mm(pp_[:, 0:128], B_[:, 1, :], B_[:, 0, :], True, True, [B_], [pp_])
                        mm(pp_[:, 256:384], identb, B_[:, 2, :], True, False, [cstb, B_], [pp_], sgc=True)
                        mm(pp_[:, 128:384], B_[:, 0, :], B_[:, 1:3, :], False, True, [B_], [pp_], sgc=True)
                        cp("act" if (h + i) % 2 else "dve", B_[:, :, :],
                           pp_.h[:, 0:384].rearrange("p (w t) -> p w t", w=3), [pp_], [B_])
                    else:
                        mm(pp_[:, 256:384], identb, B_[:, 2, :], True, False, [cstb, B_], [pp_])
                        mm(pp_[:, 256:384], B_[:, 0, :], B_[:, 2, :], False, True, [B_], [pp_])
                        cp("act" if (h + i) % 2 else "dve", B_[:, 2, :], pp_[:, 256:384], [pp_], [B_])
                    if h % 2:
                        yield
            pY = fw.psbanks[7]
            pY3 = pY.h[:, 0:384].rearrange("p (h d) -> p h d", h=6)
            for c in range(2):
                cs = slice(c * 64, c * 64 + 64)
                pX = fw.psb()
                pX3 = pX.h[:, 0:384].rearrange("p (h d) -> p h d", h=6)
                for jp in range(3):
                    mm(pX[cs, jp * 128:(jp + 1) * 128], ARb[:, jp, c * 64:c * 64 + 64], Hb[:, jp, :], True, False,
                       [ARb, Hb], [pX])
                    for h in (2 * jp, 2 * jp + 1):
                        mm(pX3[cs, h, :], ATb[cs, h, 2, cs], Vtm[cs, h * 64:(h + 1) * 64], False, h == 2 * jp + 1,
                           [ATb, Vtm], [pX])
                cp("act", Xb[cs, :, :], pX3[cs, :, :], [pX], [Xb])
                pU = fw.psb()
                pU3 = pU.h[:, 0:384].rearrange("p (h d) -> p h d", h=6)
                for h in range(6):
                    mm(pU3[cs, h, :], PQS[h][cs, 2, cs], Xb[cs, h, :], True, True, [PQS[h], Xb], [pU])
                cp("act", Ub[cs, :, :], pU3[cs, :, :], [pU], [Ub])
                yield
                for jp in range(3):
                    mm(pY[cs, jp * 128:(jp + 1) * 128], ARb[:, jp, 128 + c * 64:128 + c * 64 + 64], Hb[:, jp, :], True, False,
                       [ARb, Hb], [pY])
                    for h in (2 * jp, 2 * jp + 1):
                        mm(pY3[cs, h, :], ATb[cs, h, 1, cs], Ub[cs, h, :], False, False, [ATb, Ub], [pY])
                        mm(pY3[cs, h, :], ATb[cs, h, 3, cs], Vtm[cs, h * 64:(h + 1) * 64], False, h == 2 * jp + 1,
                           [ATb, Vtm], [pY])
                pH = fw.psb()
                for h in range(6):
                    jp, po = h // 2, (h % 2) * 64
                    o = pH[po:po + 64, jp * 64:(jp + 1) * 64]
                    mm(o, Ktm[cs, h * 64:(h + 1) * 64], Vtm[cs, h * 64:(h + 1) * 64], True, False, [Ktm, Vtm], [pH])
                    mm(o, Btm[cs, h * 64:(h + 1) * 64], Ub[cs, h, :], False, True, [Btm, Ub], [pH])
                for jp in range(3):
                    stt(Hs[:, jp, :], Hs[:, jp, :], ho['wc'][:, jp, c:c + 1], pH[:, jp * 64:(jp + 1) * 64],
                        ALU.mult, ALU.add, [Hs, ho['wc'], pH], [Hs])
                cp("act", Hb[0:64, :, 0:64], Hs[0:64, :, :], [Hs], [Hb])
                cp("act", Hb[64:128, :, 64:128], Hs[64:128, :, :], [Hs], [Hb])
                yield
            fw.op("dve", lambda e: e.tensor_reduce(out=stB2[:, 8:14], in_=pY3, axis=AX.X, op=ALU.add), [pY], [stB2])
            ts("dve", stB2[:, 8:14], stB2[:, 8:14], 1.0 / 64, None, ALU.mult, None, [stB2], [stB2])
            tt("dve", ytm[:, :, :], pY3, stB2[:, 8:14].unsqueeze(2).to_broadcast([128, 6, 64]), ALU.subtract, [pY, stB2], [ytm])
            t1y = t3.h.rearrange("p a b -> p (a b)").rearrange("p (h d) -> p h d", h=6)
            act(t1y, ytm[:, :, :], AF.Square, [ytm], [t3])
            fw.op("dve", lambda e: e.tensor_reduce(out=stB2[:, 16:22], in_=t1y, axis=AX.X, op=ALU.add), [t3], [stB2])
            ts("dve", stB2[:, 16:22], stB2[:, 16:22], 1.0 / 64, GN_EPS, ALU.mult, ALU.add, [stB2], [stB2])
            rsqrt_pool(stB2[:, 16:22], stB2)
            yield
            tt("dve", ytm[:, :, :], ytm[:, :, :], stB2[:, 16:22].unsqueeze(2).to_broadcast([128, 6, 64]), ALU.mult, [ytm, stB2], [ytm])
            yf = ytm.h.rearrange("p a b -> p (a b)")
            tt("dve", yf, yf, rows[:, R_XG:R_XG + 384], ALU.mult, [ytm, rows], [ytm])
            tt("dve", yf, yf, rows[:, R_XB:R_XB + 384], ALU.add, [ytm, rows], [ytm])
            tt("pool", t1y, Vtm.h.rearrange("p (h d) -> p h d", h=6), rks[:, 0:6].unsqueeze(2).to_broadcast([128, 6, 64]),
               ALU.mult, [Vtm, rks], [t3])
            tt("dve", ybf.h.rearrange("p (h d) -> p h d", h=6), ytm[:, :, :], t1y, ALU.add, [ytm, t3], [ybf])
            yield
            pt = fw.psb()
            ptb = pt.h.bitcast(BF16)
            for jp in range(3):
                tr(ptb[:, jp * 128:(jp + 1) * 128], ybf[:, jp * 128:(jp + 1) * 128], identb, [ybf, cstb], [pt])
            tt("dve", mixT[P][:, 2:5, :], ptb[:, 0:384].rearrange("p (j t) -> p j t", j=3), gbT[P][:, :, :],
               ALU.mult, [pt, gbT[P]], [mixT[P]])
            yield

        def gen_out(m):
            P = m % 2
            t0 = m * 128
            z = xs[m % 2]
            py = [fw.psb(), fw.psb()]
            for i in range(2):
                for k in range(8):
                    mm(py[i][:, :], mixT[P][:, k, :], wout[:, k, i * 512:(i + 1) * 512], k == 0, k == 7, [mixT[P], wout], [py[i]])
            for i in range(2):
                stt(z[:, i * 512:(i + 1) * 512], z[:, i * 512:(i + 1) * 512], float(ALPHA), py[i][:, :],
                    ALU.mult, ALU.add, [z, py[i]], [z])
                fw.op("dve", lambda e: e.bn_stats(out=stO[:, 6 * i:6 + 6 * i], in_=z[:, i * 512:(i + 1) * 512]), [z], [stO])
            yield
            fw.op("dve", lambda e: e.bn_aggr(out=stO[:, 12:14], in_=stO[:, 0:12]), [stO], [stO])
            ts("dve", stO[:, 13:14], stO[:, 13:14], LN_EPS, None, ALU.add, None, [stO], [stO])
            rsqrt_pool(stO[:, 13:14], stO)
            stt(stO[:, 14:15], stO[:, 12:13], -1.0, stO[:, 13:14], ALU.mult, ALU.mult, [stO], [stO])
            act(z[:, :], z[:, :], AF.Identity, [z, stO], [z], bias=stO[:, 14:15], scale=stO[:, 13:14])
            yield
            tt("pool", z[:, :], z[:, :], rows[:, R_LNG:R_LNG + 1024], ALU.mult, [z, rows], [z])
            tt("pool", z[:, :], z[:, :], rows[:, R_LNB:R_LNB + 1024], ALU.add, [z, rows], [z])
            fw.dma([(dst_d[t0:t0 + 128, :], z[:, :])], reads=[z], writes=[dst_d], sbuf=z)
            yield

        def run_layer():
            mk = {"proj": gen_proj, "B2": gen_B2, "B1": gen_B1, "C": gen_C, "A": gen_A, "out": gen_out}
            done, bfirst, act_ = set(), set(), []
            nxt = {n: 0 for n in mk}

            def ok(n, k):
                return k < 0 or (n, k) in done

            def eligible(n, k):
                if n == "proj":
                    return (ok("proj", k - 1) and (k - 1 < 0 or (k - 1) in bfirst) and ok("A", k - 2)
                            and ok("B2", k - 2) and ok("C", k - 2) and ok("out", k - 2))
                if n == "out":
                    return ok("A", k) and ok("B2", k) and ok("C", k) and ok("out", k - 1)
                if n == "B1":
                    return ok("proj", k) and ok("B1", k - 1) and ok("B2", k - 2)
                if n == "B2":
                    return ok("B1", k) and ok("B2", k - 1) and ok("out", k - 2)
                return ok("proj", k) and ok(n, k - 1) and ok("out", k - 2)

            while len(done) < 6 * NTT:
                for n in mk:
                    k = nxt[n]
                    if k < NTT and eligible(n, k):
                        act_.append([n, k, mk[n](k), 0.0, 0])
                        nxt[n] += 1
                i = min(range(len(act_)), key=lambda j: act_[j][3] - (BPRIO if act_[j][0] == 'B2' else 0.0))
                _NSTEP[0] += 1
                if STOPN is not None and _NSTEP[0] > STOPN:
                    raise _Stop(nc, fw)
                ent = act_[i]
                try:
                    next(ent[2])
                    ent[3] = fw.last_end
                    ent[4] += 1
                    if ent[0] == "B1" and ent[4] == 1:
                        bfirst.add(ent[1])
                except StopIteration:
                    done.add((ent[0], ent[1]))
                    act_.pop(i)

        run_layer()
    fw.barrier()
    return nc, fw


def _consts():
    cf = np.zeros((128, NCF), np.float32)
    cb = np.zeros((128, NCB), np.float32)
    s = np.arange(128)[:, None]
    t = np.arange(128)[None, :]
    same = (s // 64) == (t // 64)
    cf[:, F_ID:F_ID + 128] = np.eye(128)
    cf[:, F_ONE:F_ONE + 128] = 1.0
    cf[:, F_SEL] = (np.arange(128) < 64)
    cf[:, F_SEL + 1] = (np.arange(128) >= 64)
    cb[:, B_ID:B_ID + 128] = np.eye(128)
    mst = (same & (t > s)).astype(np.float32)
    minc = (same & (t >= s)).astype(np.float32)
    cb[:, B_M4:B_M4 + 512] = np.concatenate([mst, minc, mst, minc], 1)
    cb[:, B_P0:B_P0 + 128] = (same & (t < s)).astype(np.float32)
    cb[:, B_TRIL:B_TRIL + 128] = (s <= t).astype(np.float32)
    cb[:, B_ME:B_ME + 128] = ((s // 64) >= (t // 64)).astype(np.float32)
    cb[:, B_ME + 128:B_ME + 256] = ((s // 64) <= (t // 64)).astype(np.float32)
    sm = np.ones(384, np.float32)
    sm[::64] = 0.0
    cb[:, B_SCAN:B_SCAN + 384] = sm[None, :]
    cb[:, B_SEL] = (np.arange(128) < 64)
    cb[:, B_SEL + 1] = (np.arange(128) >= 64)
    cb[:, B_NEG:B_NEG + 256] = (cb[:, B_ME:B_ME + 256] - 1.0) * 30000.0
    return cf, cb.astype(ml_dtypes.bfloat16)


def _cols(v, n):
    return np.ascontiguousarray(np.asarray(v, np.float32).reshape(n, 128).T)


_CACHE = {}


def kernel(x, c, w_ada, b_ada, w_in, sgu_ln_g, sgu_ln_b, w_spatial, b_spatial, mu_shift, w_decay0, w_decay2,
           a0, a2, k_k, k_a, r_k, lnx_g, lnx_b, v0, v1, v2, rel_bias, w_out, ln_g, ln_b, _dbg=None, _ncores=None):
    x = np.asarray(x, np.float32)
    B, T, _ = x.shape
    L = int(np.asarray(w_in).shape[0])
    f = lambda a: np.asarray(a, np.float32)
    pvs = np.zeros((L, 128, NPV), np.float32)
    rws = np.zeros((L, 128, NROW), np.float32)
    for l in range(L):
        pvs[l, :, 0:10] = _cols(f(mu_shift)[l], 10)
        pvs[l, :, 10:13] = _cols(f(w_decay0)[l], 3)
        pvs[l, :, 13:16] = _cols(f(a0)[l], 3)
        pvs[l, :, 16:19] = _cols(f(k_k)[l], 3)
        pvs[l, :, 19:22] = _cols(f(k_a)[l], 3)
        pvs[l, :, 22:25] = _cols(f(r_k)[l].reshape(-1), 3)
        if l > 0:
            pvs[l, :, 25:28] = _cols(f(v0)[l - 1], 3)
        pvs[l, :, 28:52] = _cols(f(b_ada)[l], 24)
        rws[l, :, R_LNG:R_LNG + 1024] = f(ln_g)[l][None]
        rws[l, :, R_LNB:R_LNB + 1024] = f(ln_b)[l][None]
        rws[l, :, R_SG:R_SG + 256] = f(sgu_ln_g)[l][None]
        rws[l, :, R_SB:R_SB + 256] = f(sgu_ln_b)[l][None]
        rws[l, :, R_XG:R_XG + 384] = f(lnx_g)[l][None]
        rws[l, :, R_XB:R_XB + 384] = f(lnx_b)[l][None]
    bsp = np.ascontiguousarray(f(b_spatial).reshape(L, 1, 512))
    bgate = np.ascontiguousarray(f(b_ada)[:, None, 2048:3072])
    wsT = np.ascontiguousarray(np.transpose(f(w_spatial), (0, 3, 1, 2)).reshape(L, 128, 512))
    lora = np.ascontiguousarray(np.concatenate([f(w_decay2), f(a2)], axis=1))
    if L > 1:
        v1r = np.ascontiguousarray(f(v1)[0].reshape(3, 128, 32).transpose(1, 0, 2).reshape(128, 96))
        v2r = np.ascontiguousarray(f(v2)[0])
    else:
        v1r = np.zeros((128, 96), np.float32)
        v2r = np.zeros((32, 384), np.float32)
    kk_ = np.arange(128)[:, None, None]
    jj_ = np.arange(5)[None, :, None]
    qq_ = np.arange(128)[None, None, :]
    idx = np.clip(qq_ - kk_ + 128 * (4 - jj_), -256, 256) + 256
    bg = f(rel_bias)[:, [0, 2, 4, 1, 3, 5]][:, :, idx]
    bg = np.ascontiguousarray(np.transpose(bg, (0, 2, 3, 1, 4)).reshape(L, 128, 3840))
    cstf, cstb = _consts()
    key = (T, L, tuple(sorted(_dbg.items())) if _dbg else None)
    nc, fw = build(T, L, dbg=_dbg)
    ncores = _ncores or B
    in_maps = []
    for b in range(ncores):
        in_maps.append({
            "x": np.ascontiguousarray(x[b]), "cvec": _cols(f(c)[b], 8),
            "wada": f(w_ada), "win": f(w_in), "wout": f(w_out),
            "pv": pvs, "rows": rws, "bsp": bsp, "bgate": bgate, "wsT": wsT, "lora": lora,
            "v1r": v1r, "v2r": v2r, "biasg": bg, "cstf": cstf, "cstb": cstb,
        })
    res = run_bass_kernel_spmd(nc, in_maps, core_ids=list(range(ncores)))
    out = np.stack([np.asarray(r["out"], np.float32) for r in res.results], 0)
    if _dbg:
        return out, res.results
    return out
```

```python
import numpy as np
import ml_dtypes
import concourse.bass as bass
import concourse.mybir as mybir
from concourse.bass_utils import run_bass_kernel_spmd

F32 = mybir.dt.float32
BF16 = mybir.dt.bfloat16
AF = mybir.ActivationFunctionType
ALU = mybir.AluOpType
AX = mybir.AxisListType

D = 1024
PROJ = 3968
ALPHA = 4.0 ** 0.25
LN_EPS = 1e-5
GN_EPS = 64e-5
EPOCH = 12000
NPV = 52
R_LNG, R_LNB, R_SG, R_SB, R_XG, R_XB, NROW = 0, 1024, 2048, 2304, 2560, 2944, 3328
F_ID, F_ONE, F_SEL, NCF = 0, 128, 256, 258
B_ID, B_M4, B_P0, B_TRIL, B_ME, B_SCAN, B_SEL, B_NEG, B_ONE, NCB = 0, 128, 640, 768, 896, 1152, 1536, 1540, 1796, 1924


class Buf:
    def __init__(self, fw, name, handle, dma=False):
        self.name = name
        self.h = handle.ap() if type(handle).__name__.endswith("TensorHandle") else handle
        self.last_write = None
        self.readers = []
        self.dma_sem = fw.nc.alloc_semaphore("d_" + name) if dma else None
        self.dma_cnt = 0
        self.ready = 0.0
        self.rdone = 0.0

    def __getitem__(self, idx):
        return self.h[idx]


class EngState:
    def __init__(self, fw, name, obj):
        self.name = name
        self.obj = obj
        self.sem = fw.nc.alloc_semaphore(f"e_{name}_0")
        self.nsem = 1
        self.count = 0
        self.known = {}
        self.n_instr = 0
        self.n_wait = 0
        self.free = 0.0


class Fw:
    def __init__(self, nc):
        self.nc = nc
        self.eng = {}
        for name, obj in (("pe", nc.tensor), ("act", nc.scalar), ("dve", nc.vector),
                          ("pool", nc.gpsimd), ("sp", nc.sync)):
            self.eng[name] = EngState(self, name, obj)
        self.dma_bufs = []
        self.sb_bytes = 0
        self.psb_i = 0
        self.psbanks = []
        self.last_end = 0.0

    def sb(self, name, shape, dtype, dma=False):
        h = self.nc.alloc_sbuf_tensor("s_" + name, list(shape), dtype)
        n = 1
        for s in shape[1:]:
            n *= s
        self.sb_bytes += n * (4 if dtype == F32 else 2)
        b = Buf(self, name, h, dma)
        if dma:
            self.dma_bufs.append(b)
        return b

    def carve(self, name, ap, dma=False):
        b = Buf(self, name, ap, dma)
        if dma:
            self.dma_bufs.append(b)
        return b

    def dram(self, name, shape, dtype, kind="Internal"):
        h = self.nc.dram_tensor(name, list(shape), dtype, kind=kind)
        return Buf(self, name, h)

    def make_psum(self):
        for i in range(8):
            h = self.nc.alloc_psum_tensor(f"psb{i}", [128, 512], F32)
            self.psbanks.append(Buf(self, f"psb{i}", h))

    def psb(self):
        b = self.psbanks[self.psb_i % 6]
        self.psb_i += 1
        return b

    def _waits(self, e, reads, writes):
        need = {}

        def add(m):
            if m is None:
                return
            s, v = m
            k = s.num
            if k not in need or need[k][1] < v:
                need[k] = (s, v)

        for r in reads:
            add(r.last_write)
        for w in writes:
            add(w.last_write)
            for m in w.readers:
                add(m)
        for k, (s, v) in need.items():
            if e.name == "pe" and k == e.sem.num:
                continue
            if e.known.get(k, 0) >= v:
                continue
            e.obj.wait_ge(s, v)
            e.known[k] = v
            e.n_wait += 1

    def _mark(self, marker, reads, writes):
        for w in writes:
            w.last_write = marker
            w.readers = []
        for r in reads:
            if any(r is w for w in writes):
                continue
            r.readers.append(marker)
            if len(r.readers) > 48:
                best = {}
                for s, v in r.readers:
                    if s.num not in best or best[s.num][1] < v:
                        best[s.num] = (s, v)
                r.readers = list(best.values())

    _COST = {"pe": (60.0, 0.65), "act": (220.0, 0.6), "dve": (70.0, 1.0), "pool": (300.0, 2.2), "sp": (2500.0, 0.0)}

    def _vt(self, e, reads, writes, n):
        a, b = self._COST[e.name]
        st = e.free
        for r in reads:
            if r.ready > st:
                st = r.ready
        for w in writes:
            if w.ready > st:
                st = w.ready
            if w.rdone > st:
                st = w.rdone
        end = st + a + b * n
        e.free = end
        for w in writes:
            w.ready = end + 500.0
        for r in reads:
            if end > r.rdone:
                r.rdone = end
        self.last_end = end

    def op(self, en, fn, reads=(), writes=(), n=128):
        e = self.eng[en]
        self._vt(e, reads, writes, n)
        if e.count >= EPOCH:
            e.sem = self.nc.alloc_semaphore(f"e_{en}_{e.nsem}")
            e.nsem += 1
            e.count = 0
        self._waits(e, reads, writes)
        ins = fn(e.obj)
        e.count += 1
        e.n_instr += 1
        ins.then_inc(e.sem, 1)
        self._mark((e.sem, e.count), reads, writes)
        return ins

    def dma(self, pairs, reads=(), writes=(), sbuf=None, q="sp"):
        e = self.eng[q]
        self._vt(e, reads, writes, 0)
        self._waits(e, reads, writes)
        for o, i in pairs:
            e.obj.dma_start(out=o, in_=i).then_inc(sbuf.dma_sem, 16)
            sbuf.dma_cnt += 16
            e.n_instr += 1
        self._mark((sbuf.dma_sem, sbuf.dma_cnt), reads, writes)

    def barrier(self):
        for e in self.eng.values():
            for e2 in self.eng.values():
                if e2 is e or e2.count == 0:
                    continue
                if e.known.get(e2.sem.num, 0) < e2.count:
                    e.obj.wait_ge(e2.sem, e2.count)
                    e.known[e2.sem.num] = e2.count
            for b in self.dma_bufs:
                if b.dma_cnt and e.known.get(b.dma_sem.num, 0) < b.dma_cnt:
                    e.obj.wait_ge(b.dma_sem, b.dma_cnt)
                    e.known[b.dma_sem.num] = b.dma_cnt

    def stats(self):
        return {k: (e.n_instr, e.n_wait, e.nsem) for k, e in self.eng.items()}


class _Stop(Exception):
    pass


STOP = None
STOPN = None
import os as _os
BPRIO = float(_os.environ.get('BPRIO', '6000'))
_NSTEP = [0]


def build(T, L, NS=1, dbg=None):
    try:
        return _build(T, L, NS, dbg)
    except _Stop as e:
        nc, fw = e.args
        fw.barrier()
        return nc, fw


def _build(T, L, NS=1, dbg=None):
    nc = bass.Bass("TRN2", target_bir_lowering=False)
    fw = Fw(nc)
    _NSTEP[0] = 0
    NTT = T // 128
    assert T % 128 == 0

    x_d = fw.dram("x", [T, D], F32, "ExternalInput")
    cv_d = fw.dram("cvec", [128, 8], F32, "ExternalInput")
    wada_d = fw.dram("wada", [L, D, 3 * D], F32, "ExternalInput")
    win_d = fw.dram("win", [L, D, PROJ], F32, "ExternalInput")
    wout_d = fw.dram("wout", [L, D, D], F32, "ExternalInput")
    pv_d = fw.dram("pv", [L, 128, NPV], F32, "ExternalInput")
    rows_d = fw.dram("rows", [L, 128, NROW], F32, "ExternalInput")
    bsp_d = fw.dram("bsp", [L, 1, 512], F32, "ExternalInput")
    bgate_d = fw.dram("bgate", [L, 1, D], F32, "ExternalInput")
    wsT_d = fw.dram("wsT", [L, 128, 512], F32, "ExternalInput")
    lora_d = fw.dram("lora", [L, 128, 384], F32, "ExternalInput")
    v1_d = fw.dram("v1r", [128, 96], F32, "ExternalInput")
    v2_d = fw.dram("v2r", [32, 384], F32, "ExternalInput")
    bg_d = fw.dram("biasg", [L, 128, 3840], F32, "ExternalInput")
    cst_d = fw.dram("cstf", [128, NCF], F32, "ExternalInput")
    cstb_d = fw.dram("cstb", [128, NCB], BF16, "ExternalInput")
    out_d = fw.dram("out", [T, D], F32, "ExternalOutput")
    x1_d = fw.dram("x1s", [T, D], F32) if L > 1 else None
    vf_d = fw.dram("vfs", [384, T], F32) if L > 1 else None

    fw.make_psum()

    cst = fw.sb("cstf_s", [128, NCF], F32, dma=True)
    cstb = fw.sb("cstb_s", [128, NCB], BF16, dma=True)
    win = fw.sb("win", [128, 8, PROJ], BF16)
    wout = fw.sb("woutb", [128, 8, D], BF16)
    rows = fw.sb("rows", [128, NROW], F32, dma=True)
    pv = fw.sb("pv", [128, NPV], F32, dma=True)
    eb = fw.sb("eb", [128, 5, 6, 128], BF16)
    wsT = fw.sb("wsTb", [128, 4, 128], BF16)
    lora = fw.sb("lorab", [128, 384], BF16)
    v1b = fw.sb("v1b", [128, 3, 32], BF16)
    v2b = fw.sb("v2b", [32, 384], BF16)
    bsp = fw.sb("bsp", [1, 1024], BF16)
    cvec = fw.sb("cvec", [128, 8], F32, dma=True)
    mod = fw.sb("mod", [128, 16], F32)
    dp = fw.sb("dp", [128, 16], F32)
    mhalf = fw.sb("mhalf", [128, 8], F32)
    hT = fw.sb("hT", [128, 8, 128], BF16)
    xs = [fw.sb(f"xs{i}", [128, D], F32, dma=True) for i in range(2)]
    xb = fw.sb("xb", [128, D], BF16)
    uT = [fw.sb(f"uT{i}", [128, 2, 128], BF16) for i in range(2)]
    gaT = [fw.sb(f"gaT{i}", [128, 2, 128], BF16) for i in range(2)]
    gbT = [fw.sb(f"gbT{i}", [128, 3, 128], BF16) for i in range(2)]
    gcT = [fw.sb(f"gcT{i}", [128, 3, 128], BF16) for i in range(2)]
    qT = [fw.sb(f"qT{i}", [128, 3, 128], BF16) for i in range(2)]
    mixT = [fw.sb(f"mixT{i}", [128, 8, 128], BF16) for i in range(2)]
    kT = fw.sb("kT", [128, 3, 768], BF16)
    vaug = fw.sb("vaug", [128, 6, 6, 68], BF16)
    pB = fw.sb("pB", [128, 10, 129], F32)
    sg1s = [fw.sb(f"sg1_{i}", [128, 256], F32) for i in range(2)]
    sg2 = fw.sb("sg2", [128, 256], F32)
    vnb = fw.sb("vnb", [128, 256], BF16)
    stA = fw.sb("stA", [128, 8], F32)
    stB = fw.sb("stB", [128, 24], F32)
    stC = fw.sb("stC", [128, 8], F32)
    stO = fw.sb("stO", [128, 16], F32)
    PT = fw.sb("PT", [128, 5, 2, 3, 128], BF16)
    att = fw.sb("att", [128, 384], BF16)
    Hs = fw.sb("Hs", [128, 3, 64], F32)
    Hb = fw.sb("Hb", [128, 3, 128], BF16)
    ATb = fw.sb("ATb", [128, 6, 4, 128], BF16)
    PQS = [fw.sb(f"PQS{h}", [128, 3, 128], BF16) for h in range(6)]
    arena = nc.alloc_sbuf_tensor("arena", [128, 9216], F32)
    fw.sb_bytes += 36864
    aap = arena.ap()
    stage = [fw.carve(f"stage{i}", aap[:, i * 4096:(i + 1) * 4096], dma=True) for i in range(2)]
    g1r = fw.carve("g1r", aap[:, 8192:9216], dma=True)
    _off = [0]

    def cv(name, nfree, dtype=F32, shape3=None, dma=False):
        n32 = nfree if dtype == F32 else (nfree + 1) // 2
        ap = aap[:, _off[0]:_off[0] + n32]
        _off[0] += n32
        assert _off[0] <= 9216, name
        if dtype == BF16:
            ap = ap.bitcast(BF16)
        if shape3:
            ap = ap.rearrange("p (a b) -> p a b", a=shape3[0])
        return fw.carve(name, ap, dma)

    xsB = cv("xsB", 1280, shape3=(10, 128))
    vfT = cv("vfT", 384, shape3=(3, 128), dma=True)
    tA = cv("tA", 384, shape3=(3, 128))
    tK = cv("tK", 384, shape3=(3, 128))
    tM = cv("tM", 384, shape3=(3, 128))
    tBv = cv("tBv", 384, shape3=(3, 128))
    tL = cv("tL", 384, shape3=(3, 128))
    tC = cv("tC", 384, shape3=(3, 128))
    tEi = cv("tEi", 384, shape3=(3, 128))
    tEn = cv("tEn", 384, shape3=(3, 128))
    tEe = cv("tEe", 384, shape3=(3, 128))
    t1 = cv("t1", 384, shape3=(3, 128))
    t2 = cv("t2", 384, shape3=(3, 128))
    ytm = cv("ytm", 384, shape3=(6, 64))
    ARb = cv("ARb", 768, BF16, shape3=(3, 256))
    BKb = cv("BKb", 768, BF16, shape3=(3, 256))
    bhb = cv("bhb", 384, BF16, shape3=(3, 128))
    khb = cv("khb", 384, BF16, shape3=(3, 128))
    vbf = cv("vbf", 384, BF16, shape3=(3, 128))
    twb = cv("twb", 128, BF16)
    Vtm = cv("Vtm", 384, BF16)
    Btm = cv("Btm", 384, BF16)
    Ktm = cv("Ktm", 384, BF16)
    Xb = cv("Xb", 384, BF16, shape3=(6, 64))
    Ub = cv("Ub", 384, BF16, shape3=(6, 64))
    ybf = cv("ybf", 384, BF16)
    vv1 = cv("vv1", 128, BF16)
    rkb = cv("rkb", 384, BF16, shape3=(3, 128))
    rks = cv("rks", 8)

    def mm(out, lhsT, rhs, start, stop, R, W, sgc=False):
        fw.op("pe", lambda e: e.matmul(out, lhsT=lhsT, rhs=rhs, start=start, stop=stop, skip_group_check=sgc), R, W,
              n=rhs.free_size())

    def tr(out, in_, ident, R, W):
        fw.op("pe", lambda e: e.transpose(out, in_, ident), R, W, n=128)

    def act(out, in_, func, R, W, bias=None, scale=None):
        kw = {}
        if bias is not None:
            kw["bias"] = bias
        if scale is not None:
            kw["scale"] = scale
        fw.op("act", lambda e: e.activation(out=out, in_=in_, func=func, **kw), R, W, n=out.free_size())

    def tt(en, out, in0, in1, op, R, W):
        fw.op(en, lambda e: e.tensor_tensor(out=out, in0=in0, in1=in1, op=op), R, W, n=out.free_size())

    def ts(en, out, in0, s1, s2, op0, op1, R, W):
        if s2 is None:
            fw.op(en, lambda e: e.tensor_scalar(out=out, in0=in0, scalar1=s1, scalar2=None, op0=op0), R, W, n=out.free_size())
        else:
            fw.op(en, lambda e: e.tensor_scalar(out=out, in0=in0, scalar1=s1, scalar2=s2, op0=op0, op1=op1), R, W, n=out.free_size())

    def stt(out, in0, scalar, in1, op0, op1, R, W):
        fw.op("dve", lambda e: e.scalar_tensor_tensor(out=out, in0=in0, scalar=scalar, in1=in1, op0=op0, op1=op1), R, W, n=out.free_size())

    def cp(en, out, in_, R, W):
        if en == "act":
            fw.op("act", lambda e: e.copy(out=out, in_=in_), R, W, n=out.free_size())
        else:
            fw.op(en, lambda e: e.tensor_copy(out=out, in_=in_), R, W, n=out.free_size())

    def rsqrt_pool(ap, buf):
        n = ap.shape[1]
        fw.op("pool", lambda e: e.tensor_tensor(out=ap, in0=ap, in1=mhalf[:, 0:n], op=ALU.pow), [buf, mhalf], [buf])

    def chk(stage):
        if STOP == stage:
            raise _Stop(nc, fw)

    fw.dma([(cst[:, :], cst_d[:, :])], reads=[], writes=[cst], sbuf=cst)
    fw.dma([(cstb[:, :], cstb_d[:, :])], reads=[], writes=[cstb], sbuf=cstb)
    identf = cst[:, F_ID:F_ID + 128]
    onesf = cst[:, F_ONE:F_ONE + 128]
    sel2f = cst[:, F_SEL:F_SEL + 2]
    identb = cstb[:, B_ID:B_ID + 128]
    M4 = cstb[:, B_M4:B_M4 + 512]
    MP0 = cstb[:, B_P0:B_P0 + 128]
    tril = cstb[:, B_TRIL:B_TRIL + 128]
    maskE = cstb[:, B_ME:B_ME + 256]
    scanm = cstb[:, B_SCAN:B_SCAN + 384]
    sel2b = cstb[:, B_SEL:B_SEL + 2]
    onesb = cstb[:, B_ONE:B_ONE + 128]
    negE = cstb[:, B_NEG:B_NEG + 256]
    fw.dma([(cvec[:, :], cv_d[:, :])], reads=[], writes=[cvec], sbuf=cvec)
    fw.op("pool", lambda e: e.memset(vaug[:, :, :, 64:68], 1.0), [], [vaug])
    fw.op("pool", lambda e: e.memset(mhalf[:, :], -0.5), [], [mhalf])
    C1 = -0.5 * float(np.exp(-0.5))

    for l in range(L):
        src_d = x_d if l == 0 else x1_d
        dst_d = out_d if l == L - 1 else x1_d
        fw.barrier()
        fw.dma([(pv[:, :], pv_d[l])], writes=[pv], sbuf=pv)
        fw.dma([(rows[:, :], rows_d[l])], writes=[rows], sbuf=rows)
        sgb = stage[0]
        fw.dma([(sgb[0:1, 0:512], bsp_d[l])], writes=[sgb], sbuf=sgb)
        cp("dve", bsp[0:1, 0:512], sgb[0:1, 0:512], [sgb], [bsp])
        tt("dve", bsp[0:1, 512:1024], sgb[0:1, 0:512], bsp[0:1, 0:512], ALU.subtract, [sgb, bsp], [bsp])
        ts("dve", dp[:, 0:6], pv[:, 10:16], 0.5, None, ALU.mult, None, [pv], [dp])
        ts("dve", dp[:, 6:9], pv[:, 25:28], 0.5, None, ALU.mult, None, [pv], [dp])
        ts("dve", dp[:, 9:12], pv[:, 19:22], -1.0, 1.0, ALU.mult, ALU.add, [pv], [dp])
        act(stB[:, 8:16], cvec[:, :], AF.Tanh, [cvec], [stB], scale=0.5)
        stt(stB[:, 0:8], stB[:, 8:16], 1.0, cvec[:, :], ALU.add, ALU.mult, [stB, cvec], [stB])
        ts("dve", stB[:, 0:8], stB[:, 0:8], 0.5, None, ALU.mult, None, [stB], [stB])
        modps = fw.psb()
        growps = [fw.psb(), fw.psb()]
        for ch in range(6):
            sg = stage[ch % 2]
            sgv = sg.h.rearrange("p (k n) -> p k n", k=8)
            fw.dma([(sgv[:, k, :], wada_d[l, k * 128:(k + 1) * 128, ch * 512:(ch + 1) * 512]) for k in range(8)],
                   writes=[sg], sbuf=sg)
            if ch < 4:
                for jj in range(4):
                    j = ch * 4 + jj
                    for k in range(8):
                        mm(modps[:, j:j + 1], sgv[:, k, jj * 128:(jj + 1) * 128], stB[:, k:k + 1],
                           k == 0, k == 7, [sg, stB], [modps])
            else:
                gp = growps[ch - 4]
                for k in range(8):
                    mm(gp[0:1, :], stB[:, k:k + 1], sgv[:, k, :], k == 0, k == 7, [sg, stB], [gp])
        tt("dve", mod[:, :], modps[:, 0:16], pv[:, 28:44], ALU.add, [modps, pv], [mod])
        ts("dve", mod[:, 8:16], mod[:, 8:16], 1.0, None, ALU.add, None, [mod], [mod])
        fw.dma([(g1r[0:1, :], bgate_d[l])], writes=[g1r], sbuf=g1r)
        for i in range(2):
            tt("dve", g1r[0:1, i * 512:(i + 1) * 512], g1r[0:1, i * 512:(i + 1) * 512], growps[i][0:1, :], ALU.add,
               [g1r, growps[i]], [g1r])
        ts("dve", g1r[0:1, :], g1r[0:1, :], 1.0, 0.5, ALU.add, ALU.mult, [g1r], [g1r])
        g1bc = [fw.psb(), fw.psb()]
        for i in range(2):
            mm(g1bc[i][:, :], onesf[0:1, :], g1r[0:1, i * 512:(i + 1) * 512], True, True, [cst, g1r], [g1bc[i]])
        for i in range(2):
            sg = stage[i]
            sgv = sg.h.rearrange("p (k n) -> p k n", k=8)
            fw.dma([(sgv[:, k, :], wout_d[l, k * 128:(k + 1) * 128, i * 512:(i + 1) * 512]) for k in range(8)],
                   writes=[sg], sbuf=sg)
            for k in range(8):
                tt("dve", wout[:, k, i * 512:(i + 1) * 512], sgv[:, k, :], g1bc[i][:, :], ALU.mult,
                   [sg, g1bc[i]], [wout])
        for ch in range(8):
            sg = stage[ch % 2]
            sgv = sg.h[:, 0:3968].rearrange("p (k n) -> p k n", k=8)
            fw.dma([(sgv[:, k, :], win_d[l, k * 128:(k + 1) * 128, ch * 496:(ch + 1) * 496]) for k in range(8)],
                   writes=[sg], sbuf=sg)
            en = "act" if ch % 2 == 0 else "dve"
            cp(en, win[:, :, ch * 496:(ch + 1) * 496], sgv[:, :, :], [sg], [win])
        sg = stage[0]
        fw.dma([(sg[:, 0:512], wsT_d[l])], writes=[sg], sbuf=sg)
        tt("dve", wsT[:, :, :], sg.h[:, 0:512].rearrange("p (g n) -> p g n", g=4),
           tril.unsqueeze(1).to_broadcast([128, 4, 128]), ALU.mult, [sg, cstb], [wsT])
        fw.dma([(sg[:, 512:896], lora_d[l])], writes=[sg], sbuf=sg)
        cp("dve", lora[:, :], sg[:, 512:896], [sg], [lora])
        if l > 0:
            fw.dma([(sg[:, 896:992], v1_d[:, :]), (sg[0:32, 1024:1408], v2_d[:, :])], writes=[sg], sbuf=sg)
            cp("dve", v1b[:, :, :], sg.h[:, 896:992].rearrange("p (a b) -> p a b", a=3), [sg], [v1b])
            cp("dve", v2b[:, :], sg[0:32, 1024:1408], [sg], [v2b])
        sg = stage[1]
        fw.dma([(sg[:, 0:3840], bg_d[l])], writes=[sg], sbuf=sg)
        for jj in range(5):
            src = sg.h[:, jj * 768:(jj + 1) * 768].rearrange("p (h q) -> p h q", h=6)
            if jj in (0, 4):
                mi = 0 if jj == 0 else 1
                tt("dve", src, src, maskE[:, mi * 128:(mi + 1) * 128].unsqueeze(1).to_broadcast([128, 6, 128]), ALU.mult,
                   [sg, cstb], [sg])
                tt("dve", eb[:, jj, :, :], src, negE[:, mi * 128:(mi + 1) * 128].unsqueeze(1).to_broadcast([128, 6, 128]),
                   ALU.add, [sg, cstb], [eb])
            else:
                cp("dve", eb[:, jj, :, :], src, [sg], [eb])
        fw.barrier()
        fw.op("dve", lambda e: e.memset(Hs[:, :, :], 0.0), [], [Hs])
        fw.op("dve", lambda e: e.memset(Hb[:, :, :], 0.0), [], [Hb])
        fw.op("pool", lambda e: e.memset(pB[:, :, 0:1], 0.0), [], [pB])

        def gen_proj(m):
            t0 = m * 128
            X = xs[m % 2]
            P = m % 2
            slot = m % 6
            fw.dma([(X[:, :], src_d[t0:t0 + 128, :])], reads=[src_d], writes=[X], sbuf=X)
            yield
            cp("act", xb[:, :], X[:, :], [X], [xb])
            yield
            for k2 in range(2):
                pt = fw.psb()
                ptb = pt.h.bitcast(BF16)
                for kk in range(4):
                    k = k2 * 4 + kk
                    tr(ptb[:, kk * 128:(kk + 1) * 128], xb[:, k * 128:(k + 1) * 128], identb, [xb, cstb], [pt])
                for kk in range(4):
                    k = k2 * 4 + kk
                    act(hT[:, k, :], ptb[:, kk * 128:(kk + 1) * 128], AF.Identity, [pt, mod], [hT],
                        bias=mod[:, k:k + 1], scale=mod[:, 8 + k:9 + k])
                yield

            def proj_fm(c0, n):
                pp = fw.psb()
                for j in range(n):
                    for k in range(8):
                        mm(pp[:, j * 128:(j + 1) * 128], win[:, k, c0 + j * 128:c0 + (j + 1) * 128], hT[:, k, :],
                           k == 0, k == 7, [win, hT], [pp])
                return pp

            def gate(pp, n, dst):
                act(dst[:, :, :], pp.h[:, 0:n * 128].rearrange("p (j t) -> p j t", j=n), AF.Tanh, [pp], [dst], scale=0.5)
                stt(dst[:, :, :], dst[:, :, :], 1.0, pp.h[:, 0:n * 128].rearrange("p (j t) -> p j t", j=n),
                    ALU.add, ALU.mult, [dst, pp], [dst])

            for c0, j0, n in ((768, 0, 4), (1280, 4, 4), (1792, 8, 2)):
                pp = proj_fm(c0, n)
                cp("act", pB[:, j0:j0 + n, 1:129], pp.h[:, 0:n * 128].rearrange("p (j t) -> p j t", j=n), [pp], [pB])
                yield
            pp = proj_fm(2048, 3)
            gate(pp, 3, gbT[P])
            yield
            pp = proj_fm(2432, 3)
            act(qT[P][:, :, :], pp.h[:, 0:384].rearrange("p (j t) -> p j t", j=3), AF.Copy, [pp], [qT[P]], scale=0.125)
            yield
            pp = proj_fm(2816, 3)
            cp("act", kT[:, :, slot * 128:(slot + 1) * 128], pp.h[:, 0:384].rearrange("p (j t) -> p j t", j=3), [pp], [kT])
            yield
            pp = fw.psb()
            for k in range(8):
                mm(pp[:, 0:384], hT[:, k, :], win[:, k, 3200:3584], k == 0, k == 7, [hT, win], [pp])
            cp("act", vaug[:, slot, :, 0:64], pp.h[:, 0:384].rearrange("p (h d) -> p h d", h=6), [pp], [vaug])
            yield
            pp = proj_fm(3584, 3)
            gate(pp, 3, gcT[P])
            yield
            pp = proj_fm(0, 2)
            cp("act", uT[P][:, :, :], pp.h[:, 0:256].rearrange("p (j t) -> p j t", j=2), [pp], [uT[P]])
            yield
            pp = proj_fm(512, 2)
            gate(pp, 2, gaT[P])
            yield
            pa = fw.psb()
            for k in range(8):
                mm(pa[:, 0:256], hT[:, k, :], win[:, k, 256:512], k == 0, k == 7, [hT, win], [pa])
            cp("act", sg1s[P][:, :], pa[:, 0:256], [pa], [sg1s[P]])
            yield

        def gen_A(m):
            P = m % 2
            sg1 = sg1s[P]
            av3 = sg1.h.rearrange("p (g c) -> p g c", g=4)
            s23 = sg2.h.rearrange("p (g c) -> p g c", g=4)
            fw.op("dve", lambda e: e.tensor_reduce(out=stA[:, 0:4], in_=av3, axis=AX.X, op=ALU.add), [sg1], [stA])
            ts("dve", stA[:, 0:4], stA[:, 0:4], 1.0 / 64, None, ALU.mult, None, [stA], [stA])
            tt("dve", av3, av3, stA[:, 0:4].unsqueeze(2).to_broadcast([128, 4, 64]), ALU.subtract, [sg1, stA], [sg1])
            yield
            act(sg2[:, :], sg1[:, :], AF.Square, [sg1], [sg2])
            fw.op("dve", lambda e: e.tensor_reduce(out=stA[:, 4:8], in_=s23, axis=AX.X, op=ALU.add), [sg2], [stA])
            ts("dve", stA[:, 4:8], stA[:, 4:8], 1.0 / 64, LN_EPS, ALU.mult, ALU.add, [stA], [stA])
            rsqrt_pool(stA[:, 4:8], stA)
            yield
            tt("dve", av3, av3, stA[:, 4:8].unsqueeze(2).to_broadcast([128, 4, 64]), ALU.mult, [sg1, stA], [sg1])
            tt("pool", sg1[:, :], sg1[:, :], rows[:, R_SG:R_SG + 256], ALU.mult, [sg1, rows], [sg1])
            tt("pool", vnb[:, :], sg1[:, :], rows[:, R_SB:R_SB + 256], ALU.add, [sg1, rows], [vnb])
            yield
            psa = fw.psb()
            for g in range(4):
                o = psa[(g % 2) * 64:(g % 2) * 64 + 64, (g // 2) * 128:(g // 2) * 128 + 128]
                mm(o, vnb[:, g * 64:(g + 1) * 64], wsT[:, g, :], True, False, [vnb, wsT], [psa])
                mm(o, onesb[0:1, 0:64], bsp[0:1, g * 128:(g + 1) * 128], False, False, [cstb, bsp], [psa])
                mm(o, onesb[0:1, 0:64], bsp[0:1, 512 + g * 128:512 + (g + 1) * 128], False, True, [cstb, bsp], [psa])
            s2v = sg2.h.rearrange("p (j t) -> p j t", j=2)
            tt("dve", s2v, psa.h[:, 0:256].rearrange("p (j t) -> p j t", j=2), uT[P][:, :, :], ALU.mult, [psa, uT[P]], [sg2])
            tt("pool", mixT[P][:, 0:2, :], s2v, gaT[P][:, :, :], ALU.mult, [sg2, gaT[P]], [mixT[P]])
            yield

        def gen_C(m):
            P = m % 2
            jlo = max(0, m - 4)
            po_ = fw.psbanks[6]
            o3 = po_.h[:, 0:408].rearrange("p (h d) -> p h d", h=6)
            for j in range(jlo, m + 1):
                jj = j - (m - 4)
                ks = (j % 6) * 128
                pS = [fw.psb(), fw.psb()]
                ebv = eb.h.rearrange("p j h q -> p j (h q)")
                for par in range(2):
                    mm(pS[par][:, 0:384], identb, ebv[:, jj, par * 384:(par + 1) * 384], True, False, [cstb, eb], [pS[par]])
                for h3 in range(3):
                    for par in range(2):
                        po = par * 64
                        mm(pS[par][:, h3 * 128:(h3 + 1) * 128], kT[po:po + 64, h3, ks:ks + 128],
                           qT[P][po:po + 64, h3, :], False, h3 == 2, [kT, qT[P]], [pS[par]])
                for par in range(2):
                    act(PT[:, jj, par, :, :], pS[par].h[:, 0:384].rearrange("p (h q) -> p h q", h=3), AF.Exp, [pS[par]], [PT])
                yield
            for h in range(6):
                par, h3 = h % 2, h // 2
                for j in range(jlo, m + 1):
                    jj = j - (m - 4)
                    mm(o3[:, h, :], PT[:, jj, par, h3, :], vaug[:, j % 6, h, :], j == jlo, j == m, [PT, vaug], [po_])
                if h % 2:
                    yield
            fw.op("dve", lambda e: e.reciprocal(out=stC[:, 0:6], in_=o3[:, :, 64]), [po_], [stC])
            tt("dve", att.h.rearrange("p (h d) -> p h d", h=6), o3[:, :, 0:64],
               stC[:, 0:6].unsqueeze(2).to_broadcast([128, 6, 64]), ALU.mult, [po_, stC], [att])
            yield
            pt = fw.psb()
            ptb = pt.h.bitcast(BF16)
            for jp in range(3):
                tr(ptb[:, jp * 128:(jp + 1) * 128], att[:, jp * 128:(jp + 1) * 128], identb, [att, cstb], [pt])
            tt("dve", mixT[P][:, 5:8, :], ptb[:, 0:384].rearrange("p (j t) -> p j t", j=3), gcT[P][:, :, :],
               ALU.mult, [pt, gcT[P]], [mixT[P]])
            yield

        def gen_B(m):
            P = m % 2
            t0 = m * 128
            W3 = pB[:, :, 1:129]
            Wp3 = pB[:, :, 0:128]
            tt("dve", xsB[:, :, :], Wp3, W3, ALU.subtract, [pB], [xsB])
            tt("dve", xsB[:, :, :], xsB[:, :, :], pv[:, 0:10].unsqueeze(2).to_broadcast([128, 10, 128]), ALU.mult,
               [xsB, pv], [xsB])
            tt("pool", xsB[:, :, :], xsB[:, :, :], W3, ALU.add, [xsB, pB], [xsB])
            cp("pool", pB[:, :, 0:1], pB[:, :, 128:129], [pB], [pB])
            yield
            r3 = xsB[:, 0:3, :]
            k3 = xsB[:, 3:6, :]
            v3 = xsB[:, 6:9, :]
            if L > 1 and l == 0:
                fw.dma([(vf_d[jp * 128:(jp + 1) * 128, t0:t0 + 128], xsB[:, 6 + jp, :])
                        for jp in range(3)], reads=[xsB], writes=[vf_d], sbuf=vfT)
            if l > 0:
                fw.dma([(vfT[:, jp, :], vf_d[jp * 128:(jp + 1) * 128, t0:t0 + 128])
                        for jp in range(3)], reads=[vf_d], writes=[vfT], sbuf=vfT)
            act(twb[0:64, :], xsB[0:64, 9, :], AF.Tanh, [xsB], [twb])
            cp("act", twb[64:128, :], xsB[64:128, 9, :], [xsB], [twb])
            yield
            pz = fw.psb()
            pa2 = fw.psb()
            for jp in range(3):
                mm(pz[:, jp * 128:(jp + 1) * 128], lora[0:64, jp * 128:(jp + 1) * 128], twb[0:64, :], True, True,
                   [lora, twb], [pz])
                mm(pa2[:, jp * 128:(jp + 1) * 128], lora[64:128, jp * 128:(jp + 1) * 128], twb[64:128, :], True, True,
                   [lora, twb], [pa2])
            for jp in range(3):
                act(tL[:, jp, :], pz[:, jp * 128:(jp + 1) * 128], AF.Tanh, [pz, dp], [tL], bias=dp[:, jp:jp + 1], scale=0.5)
                act(tA[:, jp, :], pa2[:, jp * 128:(jp + 1) * 128], AF.Tanh, [pa2, dp], [tA], bias=dp[:, 3 + jp:4 + jp], scale=0.5)
            ts("dve", tL[:, :, :], tL[:, :, :], C1, C1, ALU.mult, ALU.add, [tL], [tL])
            ts("dve", tA[:, :, :], tA[:, :, :], 0.5, 0.5, ALU.mult, ALU.add, [tA], [tA])
            yield
            if l > 0:
                cp("act", vbf[:, :, :], v3, [xsB], [vbf])
                pv1 = fw.psb()
                for jp in range(3):
                    mm(pv1[0:32, 0:128], v1b[:, jp, :], vbf[:, jp, :], jp == 0, jp == 2, [v1b, vbf], [pv1])
                cp("act", vv1[0:32, :], pv1[0:32, 0:128], [pv1], [vv1])
                pv2 = fw.psb()
                for jp in range(3):
                    mm(pv2[:, jp * 128:(jp + 1) * 128], v2b[0:32, jp * 128:(jp + 1) * 128], vv1[0:32, :], True, True,
                       [v2b, vv1], [pv2])
                for jp in range(3):
                    act(t1[:, jp, :], pv2[:, jp * 128:(jp + 1) * 128], AF.Tanh, [pv2, dp], [t1],
                        bias=dp[:, 6 + jp:7 + jp], scale=0.5)
                tt("dve", t2[:, :, :], vfT[:, :, :], v3, ALU.subtract, [vfT, xsB], [t2])
                stt(t2[:, :, :], t1[:, :, :], 1.0, t2[:, :, :], ALU.add, ALU.mult, [t1, t2], [t2])
                stt(v3, t2[:, :, :], 0.5, v3, ALU.mult, ALU.add, [t2, xsB], [xsB])
                yield
            cp("act", vbf[:, :, :], v3, [xsB], [vbf])
            tt("dve", tK[:, :, :], k3, pv[:, 16:19].unsqueeze(2).to_broadcast([128, 3, 128]), ALU.mult, [xsB, pv], [tK])
            act(rkb[:, :, :], tK[:, :, :], AF.Square, [tK], [rkb])
            yield
            pss = fw.psb()
            for jp in range(3):
                mm(pss[:, jp * 2:jp * 2 + 2], rkb[:, jp, :], sel2b, True, True, [rkb, cstb], [pss])
            ts("dve", stB[:, 0:6], pss[:, 0:6], 1e-24, None, ALU.max, None, [pss], [stB])
            rsqrt_pool(stB[:, 0:6], stB)
            yield
            cp("dve", ybf.h.rearrange("p (h d) -> p h d", h=6), stB[:, 0:6].unsqueeze(2).to_broadcast([128, 6, 64]), [stB], [ybf])
            prn = fw.psb()
            prnb = prn.h.bitcast(BF16)
            for jp in range(3):
                tr(prnb[:, jp * 128:(jp + 1) * 128], ybf[:, jp * 128:(jp + 1) * 128], identb, [ybf, cstb], [prn])
            tt("dve", tK[:, :, :], tK[:, :, :], prnb[:, 0:384].rearrange("p (a b) -> p a b", a=3), ALU.mult, [tK, prn], [tK])
            yield
            for jp in range(3):
                ts("dve", tM[:, jp, :], tA[:, jp, :], pv[:, 19 + jp:20 + jp], dp[:, 9 + jp:10 + jp], ALU.mult, ALU.add, [tA, pv, dp], [tM])
            tt("dve", tM[:, :, :], tM[:, :, :], k3, ALU.mult, [tM, xsB], [tM])
            tt("dve", tBv[:, :, :], tK[:, :, :], tA[:, :, :], ALU.mult, [tK, tA], [tBv])
            yield
            tt("pool", t1[:, :, :], r3, tM[:, :, :], ALU.mult, [xsB, tM], [t1])
            tt("pool", rkb[:, :, :], t1[:, :, :], pv[:, 22:25].unsqueeze(2).to_broadcast([128, 3, 128]), ALU.mult, [t1, pv], [rkb])
            prk = fw.psb()
            for jp in range(3):
                mm(prk[:, jp * 2:jp * 2 + 2], rkb[:, jp, :], sel2b, True, True, [rkb, cstb], [prk])
            cp("act", rks[:, 0:6], prk[:, 0:6], [prk], [rks])
            yield
            tCf = tC.h.rearrange("p a b -> p (a b)")
            tLf = tL.h.rearrange("p a b -> p (a b)")
            fw.op("dve", lambda e: e.tensor_tensor_scan(out=tCf, data0=scanm, data1=tLf, initial=0.0,
                                                        op0=ALU.mult, op1=ALU.add), [cstb, tL], [tC])
            act(tEi[:, :, :], tC[:, :, :], AF.Exp, [tC], [tEi])
            act(tEn[:, :, :], tC[:, :, :], AF.Exp, [tC], [tEn], scale=-1.0)
            tt("dve", t1[:, :, :], tC[:, :, :], tL[:, :, :], ALU.subtract, [tC, tL], [t1])
            act(t1[:, :, :], t1[:, :, :], AF.Exp, [t1], [t1])
            yield
            ei4 = tEi.h.rearrange("p a (c t) -> p a c t", c=2)
            tt("dve", tEe.h.rearrange("p a (c t) -> p a c t", c=2), tEn.h.rearrange("p a (c t) -> p a c t", c=2),
               ei4[:, :, :, 63:64].to_broadcast([128, 3, 2, 64]), ALU.mult, [tEn, tEi], [tEe])
            AR4 = ARb.h.rearrange("p a (w t) -> p a w t", w=2)
            BK4 = BKb.h.rearrange("p a (w t) -> p a w t", w=2)
            stt(AR4[:, :, 0, :], tK[:, :, :], -1.0, t1[:, :, :], ALU.mult, ALU.mult, [tK, t1], [ARb])
            tt("dve", AR4[:, :, 1, :], r3, tEi[:, :, :], ALU.mult, [xsB, tEi], [ARb])
            tt("dve", BK4[:, :, 0, :], tBv[:, :, :], tEn[:, :, :], ALU.mult, [tBv, tEn], [BKb])
            tt("dve", BK4[:, :, 1, :], tM[:, :, :], tEn[:, :, :], ALU.mult, [tM, tEn], [BKb])
            yield
            tt("dve", bhb[:, :, :], tBv[:, :, :], tEe[:, :, :], ALU.mult, [tBv, tEe], [bhb])
            tt("pool", khb[:, :, :], tM[:, :, :], tEe[:, :, :], ALU.mult, [tM, tEe], [khb])
            for h in range(6):
                jp, po = h // 2, (h % 2) * 64
                pA = fw.psb()
                mm(pA[:, 0:256], BKb[po:po + 64, jp, 0:128], ARb[po:po + 64, jp, :], True, True, [BKb, ARb], [pA])
                mm(pA[:, 256:512], BKb[po:po + 64, jp, 128:256], ARb[po:po + 64, jp, :], True, True, [BKb, ARb], [pA])
                tt("dve", ATb[:, h, 1:4, :], pA.h[:, 128:512].rearrange("p (w t) -> p w t", w=3),
                   M4[:, 128:512].rearrange("p (w t) -> p w t", w=3), ALU.mult, [pA, cstb], [ATb])
                tt("dve", PQS[h][:, 1, :], pA[:, 0:128], M4[:, 0:128], ALU.mult, [pA, cstb], [PQS[h]])
                pP = fw.psb()
                mm(pP[:, 0:128], ARb[po:po + 64, jp, 0:128], BKb[po:po + 64, jp, 0:128], True, True, [ARb, BKb], [pP])
                tt("dve", PQS[h][:, 0, :], pP[:, 0:128], MP0, ALU.mult, [pP, cstb], [PQS[h]])
                cp("act", PQS[h][:, 2, :], identb, [cstb], [PQS[h]])
                if h % 2:
                    yield
            yield
            for srcb, dstb in ((vbf, Vtm), (bhb, Btm), (khb, Ktm)):
                pt = fw.psb()
                ptb = pt.h.bitcast(BF16)
                for jp in range(3):
                    tr(ptb[:, jp * 128:(jp + 1) * 128], srcb[:, jp, :], identb, [srcb, cstb], [pt])
                cp("act", dstb[:, :], ptb[:, 0:384], [pt], [dstb])
            yield
            for i in range(6):
                for h in range(6):
                    B_ = PQS[h]
                    pp_ = fw.psb()
                    if i < 5:
                        mm(pp_[:, 0:128], B_[:, 1, :], B_[:, 0, :], True, True, [B_], [pp_])
                        mm(pp_[:, 256:384], identb, B_[:, 2, :], True, False, [cstb, B_], [pp_], sgc=True)
                        mm(pp_[:, 128:384], B_[:, 0, :], B_[:, 1:3, :], False, True, [B_], [pp_], sgc=True)
                        cp("act" if (h + i) % 2 else "dve", B_[:, :, :],
                           pp_.h[:, 0:384].rearrange("p (w t) -> p w t", w=3), [pp_], [B_])
                    else:
                        mm(pp_[:, 256:384], identb, B_[:, 2, :], True, False, [cstb, B_], [pp_])
                        mm(pp_[:, 256:384], B_[:, 0, :], B_[:, 2, :], False, True, [B_], [pp_])
                        cp("act" if (h + i) % 2 else "dve", B_[:, 2, :], pp_[:, 256:384], [pp_], [B_])
                    if h % 2:
                        yield
            pY = fw.psbanks[7]
            pY3 = pY.h[:, 0:384].rearrange("p (h d) -> p h d", h=6)
            for c in range(2):
                cs = slice(c * 64, c * 64 + 64)
                pX = fw.psb()
                pX3 = pX.h[:, 0:384].rearrange("p (h d) -> p h d", h=6)
                for jp in range(3):
                    mm(pX[cs, jp * 128:(jp + 1) * 128], ARb[:, jp, c * 64:c * 64 + 64], Hb[:, jp, :], True, False,
                       [ARb, Hb], [pX])
                    for h in (2 * jp, 2 * jp + 1):
                        mm(pX3[cs, h, :], ATb[cs, h, 2, cs], Vtm[cs, h * 64:(h + 1) * 64], False, h == 2 * jp + 1,
                           [ATb, Vtm], [pX])
                cp("act", Xb[cs, :, :], pX3[cs, :, :], [pX], [Xb])
                yield
                pU = fw.psb()
                pU3 = pU.h[:, 0:384].rearrange("p (h d) -> p h d", h=6)
                for h in range(6):
                    mm(pU3[cs, h, :], PQS[h][cs, 2, cs], Xb[cs, h, :], True, True, [PQS[h], Xb], [pU])
                cp("act", Ub[cs, :, :], pU3[cs, :, :], [pU], [Ub])
                yield
                for jp in range(3):
                    mm(pY[cs, jp * 128:(jp + 1) * 128], ARb[:, jp, 128 + c * 64:128 + c * 64 + 64], Hb[:, jp, :], True, False,
                       [ARb, Hb], [pY])
                    for h in (2 * jp, 2 * jp + 1):
                        mm(pY3[cs, h, :], ATb[cs, h, 1, cs], Ub[cs, h, :], False, False, [ATb, Ub], [pY])
                        mm(pY3[cs, h, :], ATb[cs, h, 3, cs], Vtm[cs, h * 64:(h + 1) * 64], False, h == 2 * jp + 1,
                           [ATb, Vtm], [pY])
                pH = fw.psb()
                for h in range(6):
                    jp, po = h // 2, (h % 2) * 64
                    o = pH[po:po + 64, jp * 64:(jp + 1) * 64]
                    mm(o, Ktm[cs, h * 64:(h + 1) * 64], Vtm[cs, h * 64:(h + 1) * 64], True, False, [Ktm, Vtm], [pH])
                    mm(o, Btm[cs, h * 64:(h + 1) * 64], Ub[cs, h, :], False, True, [Btm, Ub], [pH])
                for jp in range(3):
                    stt(Hs[:, jp, :], Hs[:, jp, :], tEi[:, jp, c * 64 + 63:c * 64 + 64], pH[:, jp * 64:(jp + 1) * 64],
                        ALU.mult, ALU.add, [Hs, tEi, pH], [Hs])
                cp("act", Hb[0:64, :, 0:64], Hs[0:64, :, :], [Hs], [Hb])
                cp("act", Hb[64:128, :, 64:128], Hs[64:128, :, :], [Hs], [Hb])
                yield
            fw.op("dve", lambda e: e.tensor_reduce(out=stB[:, 8:14], in_=pY3, axis=AX.X, op=ALU.add), [pY], [stB])
            ts("dve", stB[:, 8:14], stB[:, 8:14], 1.0 / 64, None, ALU.mult, None, [stB], [stB])
            tt("dve", ytm[:, :, :], pY3, stB[:, 8:14].unsqueeze(2).to_broadcast([128, 6, 64]), ALU.subtract, [pY, stB], [ytm])
            t1y = t1.h.rearrange("p a b -> p (a b)").rearrange("p (h d) -> p h d", h=6)
            act(t1y, ytm[:, :, :], AF.Square, [ytm], [t1])
            fw.op("dve", lambda e: e.tensor_reduce(out=stB[:, 16:22], in_=t1y, axis=AX.X, op=ALU.add), [t1], [stB])
            ts("dve", stB[:, 16:22], stB[:, 16:22], 1.0 / 64, GN_EPS, ALU.mult, ALU.add, [stB], [stB])
            rsqrt_pool(stB[:, 16:22], stB)
            yield
            tt("dve", ytm[:, :, :], ytm[:, :, :], stB[:, 16:22].unsqueeze(2).to_broadcast([128, 6, 64]), ALU.mult, [ytm, stB], [ytm])
            yf = ytm.h.rearrange("p a b -> p (a b)")
            tt("dve", yf, yf, rows[:, R_XG:R_XG + 384], ALU.mult, [ytm, rows], [ytm])
            tt("dve", yf, yf, rows[:, R_XB:R_XB + 384], ALU.add, [ytm, rows], [ytm])
            tt("pool", t1y, Vtm.h.rearrange("p (h d) -> p h d", h=6), rks[:, 0:6].unsqueeze(2).to_broadcast([128, 6, 64]),
               ALU.mult, [Vtm, rks], [t1])
            tt("dve", ybf.h.rearrange("p (h d) -> p h d", h=6), ytm[:, :, :], t1y, ALU.add, [ytm, t1], [ybf])
            yield
            pt = fw.psb()
            ptb = pt.h.bitcast(BF16)
            for jp in range(3):
                tr(ptb[:, jp * 128:(jp + 1) * 128], ybf[:, jp * 128:(jp + 1) * 128], identb, [ybf, cstb], [pt])
            tt("dve", mixT[P][:, 2:5, :], ptb[:, 0:384].rearrange("p (j t) -> p j t", j=3), gbT[P][:, :, :],
               ALU.mult, [pt, gbT[P]], [mixT[P]])
            yield

        def gen_out(m):
            P = m % 2
            t0 = m * 128
            z = xs[m % 2]
            py = [fw.psb(), fw.psb()]
            for i in range(2):
                for k in range(8):
                    mm(py[i][:, :], mixT[P][:, k, :], wout[:, k, i * 512:(i + 1) * 512], k == 0, k == 7, [mixT[P], wout], [py[i]])
            for i in range(2):
                stt(z[:, i * 512:(i + 1) * 512], z[:, i * 512:(i + 1) * 512], float(ALPHA), py[i][:, :],
                    ALU.mult, ALU.add, [z, py[i]], [z])
                fw.op("dve", lambda e: e.bn_stats(out=stO[:, 6 * i:6 + 6 * i], in_=z[:, i * 512:(i + 1) * 512]), [z], [stO])
            yield
            fw.op("dve", lambda e: e.bn_aggr(out=stO[:, 12:14], in_=stO[:, 0:12]), [stO], [stO])
            ts("dve", stO[:, 13:14], stO[:, 13:14], LN_EPS, None, ALU.add, None, [stO], [stO])
            rsqrt_pool(stO[:, 13:14], stO)
            stt(stO[:, 14:15], stO[:, 12:13], -1.0, stO[:, 13:14], ALU.mult, ALU.mult, [stO], [stO])
            act(z[:, :], z[:, :], AF.Identity, [z, stO], [z], bias=stO[:, 14:15], scale=stO[:, 13:14])
            yield
            tt("pool", z[:, :], z[:, :], rows[:, R_LNG:R_LNG + 1024], ALU.mult, [z, rows], [z])
            tt("pool", z[:, :], z[:, :], rows[:, R_LNB:R_LNB + 1024], ALU.add, [z, rows], [z])
            fw.dma([(dst_d[t0:t0 + 128, :], z[:, :])], reads=[z], writes=[dst_d], sbuf=z)
            yield

        def run_layer():
            mk = {"proj": gen_proj, "B": gen_B, "C": gen_C, "A": gen_A, "out": gen_out}
            done, bfirst, act_ = set(), set(), []
            nxt = {n: 0 for n in mk}

            def ok(n, k):
                return k < 0 or (n, k) in done

            def eligible(n, k):
                if n == "proj":
                    return (ok("proj", k - 1) and (k - 1 < 0 or (k - 1) in bfirst) and ok("A", k - 2)
                            and ok("B", k - 2) and ok("C", k - 2) and ok("out", k - 2))
                if n == "out":
                    return ok("A", k) and ok("B", k) and ok("C", k) and ok("out", k - 1)
                return ok("proj", k) and ok(n, k - 1) and ok("out", k - 2)

            while len(done) < 5 * NTT:
                for n in mk:
                    k = nxt[n]
                    if k < NTT and eligible(n, k):
                        act_.append([n, k, mk[n](k), 0.0, 0])
                        nxt[n] += 1
                i = min(range(len(act_)), key=lambda j: act_[j][3] - (BPRIO if act_[j][0] == 'B' else 0.0))
                _NSTEP[0] += 1
                if STOPN is not None and _NSTEP[0] > STOPN:
                    raise _Stop(nc, fw)
                ent = act_[i]
                try:
                    next(ent[2])
                    ent[3] = fw.last_end
                    ent[4] += 1
                    if ent[0] == "B" and ent[4] == 1:
                        bfirst.add(ent[1])
                except StopIteration:
                    done.add((ent[0], ent[1]))
                    act_.pop(i)

        run_layer()
    fw.barrier()
    return nc, fw


def _consts():
    cf = np.zeros((128, NCF), np.float32)
    cb = np.zeros((128, NCB), np.float32)
    s = np.arange(128)[:, None]
    t = np.arange(128)[None, :]
    same = (s // 64) == (t // 64)
    cf[:, F_ID:F_ID + 128] = np.eye(128)
    cf[:, F_ONE:F_ONE + 128] = 1.0
    cf[:, F_SEL] = (np.arange(128) < 64)
    cf[:, F_SEL + 1] = (np.arange(128) >= 64)
    cb[:, B_ID:B_ID + 128] = np.eye(128)
    mst = (same & (t > s)).astype(np.float32)
    minc = (same & (t >= s)).astype(np.float32)
    cb[:, B_M4:B_M4 + 512] = np.concatenate([mst, minc, mst, minc], 1)
    cb[:, B_P0:B_P0 + 128] = (same & (t < s)).astype(np.float32)
    cb[:, B_TRIL:B_TRIL + 128] = (s <= t).astype(np.float32)
    cb[:, B_ME:B_ME + 128] = ((s // 64) >= (t // 64)).astype(np.float32)
    cb[:, B_ME + 128:B_ME + 256] = ((s // 64) <= (t // 64)).astype(np.float32)
    sm = np.ones(384, np.float32)
    sm[::64] = 0.0
    cb[:, B_SCAN:B_SCAN + 384] = sm[None, :]
    cb[:, B_SEL] = (np.arange(128) < 64)
    cb[:, B_SEL + 1] = (np.arange(128) >= 64)
    cb[:, B_NEG:B_NEG + 256] = (cb[:, B_ME:B_ME + 256] - 1.0) * 30000.0
    cb[:, B_ONE:B_ONE + 128] = 1.0
    return cf, cb.astype(ml_dtypes.bfloat16)


def _cols(v, n):
    return np.ascontiguousarray(np.asarray(v, np.float32).reshape(n, 128).T)


_CACHE = {}


def kernel(x, c, w_ada, b_ada, w_in, sgu_ln_g, sgu_ln_b, w_spatial, b_spatial, mu_shift, w_decay0, w_decay2,
           a0, a2, k_k, k_a, r_k, lnx_g, lnx_b, v0, v1, v2, rel_bias, w_out, ln_g, ln_b, _dbg=None, _ncores=None):
    x = np.asarray(x, np.float32)
    B, T, _ = x.shape
    L = int(np.asarray(w_in).shape[0])
    f = lambda a: np.asarray(a, np.float32)
    pvs = np.zeros((L, 128, NPV), np.float32)
    rws = np.zeros((L, 128, NROW), np.float32)
    for l in range(L):
        pvs[l, :, 0:10] = _cols(f(mu_shift)[l], 10)
        pvs[l, :, 10:13] = _cols(f(w_decay0)[l], 3)
        pvs[l, :, 13:16] = _cols(f(a0)[l], 3)
        pvs[l, :, 16:19] = _cols(f(k_k)[l], 3)
        pvs[l, :, 19:22] = _cols(f(k_a)[l], 3)
        pvs[l, :, 22:25] = _cols(f(r_k)[l].reshape(-1), 3)
        if l > 0:
            pvs[l, :, 25:28] = _cols(f(v0)[l - 1], 3)
        pvs[l, :, 28:52] = _cols(f(b_ada)[l], 24)
        rws[l, :, R_LNG:R_LNG + 1024] = f(ln_g)[l][None]
        rws[l, :, R_LNB:R_LNB + 1024] = f(ln_b)[l][None]
        rws[l, :, R_SG:R_SG + 256] = f(sgu_ln_g)[l][None]
        rws[l, :, R_SB:R_SB + 256] = f(sgu_ln_b)[l][None]
        rws[l, :, R_XG:R_XG + 384] = f(lnx_g)[l][None]
        rws[l, :, R_XB:R_XB + 384] = f(lnx_b)[l][None]
    bsp = np.ascontiguousarray(f(b_spatial).reshape(L, 1, 512))
    bgate = np.ascontiguousarray(f(b_ada)[:, None, 2048:3072])
    wsT = np.ascontiguousarray(np.transpose(f(w_spatial), (0, 3, 1, 2)).reshape(L, 128, 512))
    lora = np.ascontiguousarray(np.concatenate([f(w_decay2), f(a2)], axis=1))
    if L > 1:
        v1r = np.ascontiguousarray(f(v1)[0].reshape(3, 128, 32).transpose(1, 0, 2).reshape(128, 96))
        v2r = np.ascontiguousarray(f(v2)[0])
    else:
        v1r = np.zeros((128, 96), np.float32)
        v2r = np.zeros((32, 384), np.float32)
    kk_ = np.arange(128)[:, None, None]
    jj_ = np.arange(5)[None, :, None]
    qq_ = np.arange(128)[None, None, :]
    idx = np.clip(qq_ - kk_ + 128 * (4 - jj_), -256, 256) + 256
    bg = f(rel_bias)[:, [0, 2, 4, 1, 3, 5]][:, :, idx]
    bg = np.ascontiguousarray(np.transpose(bg, (0, 2, 3, 1, 4)).reshape(L, 128, 3840))
    cstf, cstb = _consts()
    key = (T, L, tuple(sorted(_dbg.items())) if _dbg else None)
    nc, fw = build(T, L, dbg=_dbg)
    ncores = _ncores or B
    in_maps = []
    for b in range(ncores):
        in_maps.append({
            "x": np.ascontiguousarray(x[b]), "cvec": _cols(f(c)[b], 8),
            "wada": f(w_ada), "win": f(w_in), "wout": f(w_out),
            "pv": pvs, "rows": rws, "bsp": bsp, "bgate": bgate, "wsT": wsT, "lora": lora,
            "v1r": v1r, "v2r": v2r, "biasg": bg, "cstf": cstf, "cstb": cstb,
        })
    res = run_bass_kernel_spmd(nc, in_maps, core_ids=list(range(ncores)))
    out = np.stack([np.asarray(r["out"], np.float32) for r in res.results], 0)
    if _dbg:
        return out, res.results
    return out
```

```python
import numpy as np
import ml_dtypes
import concourse.bass as bass
import concourse.mybir as mybir
from concourse.bass_utils import run_bass_kernel_spmd

F32 = mybir.dt.float32
BF16 = mybir.dt.bfloat16
AF = mybir.ActivationFunctionType
ALU = mybir.AluOpType
AX = mybir.AxisListType

D = 1024
PROJ = 3968
ALPHA = 4.0 ** 0.25
LN_EPS = 1e-5
GN_EPS = 64e-5
EPOCH = 12000
NPV = 52
R_LNG, R_LNB, R_SG, R_SB, R_XG, R_XB, NROW = 0, 1024, 2048, 2304, 2560, 2944, 3328
F_ID, F_ONE, F_SEL, NCF = 0, 128, 256, 258
B_ID, B_M4, B_P0, B_TRIL, B_ME, B_SCAN, B_SEL, B_NEG, B_ONE, NCB = 0, 128, 640, 768, 896, 1152, 1536, 1540, 1796, 1924


class Buf:
    def __init__(self, fw, name, handle, dma=False):
        self.name = name
        self.h = handle.ap() if type(handle).__name__.endswith("TensorHandle") else handle
        self.last_write = None
        self.readers = []
        self.dma_sem = fw.nc.alloc_semaphore("d_" + name) if dma else None
        self.dma_cnt = 0
        self.ready = 0.0
        self.rdone = 0.0

    def __getitem__(self, idx):
        return self.h[idx]


class EngState:
    def __init__(self, fw, name, obj):
        self.name = name
        self.obj = obj
        self.sem = fw.nc.alloc_semaphore(f"e_{name}_0")
        self.nsem = 1
        self.count = 0
        self.known = {}
        self.n_instr = 0
        self.n_wait = 0
        self.free = 0.0


class Fw:
    def __init__(self, nc):
        self.nc = nc
        self.eng = {}
        for name, obj in (("pe", nc.tensor), ("act", nc.scalar), ("dve", nc.vector),
                          ("pool", nc.gpsimd), ("sp", nc.sync)):
            self.eng[name] = EngState(self, name, obj)
        self.dma_bufs = []
        self.sb_bytes = 0
        self.psb_i = 0
        self.psbanks = []
        self.last_end = 0.0

    def sb(self, name, shape, dtype, dma=False):
        h = self.nc.alloc_sbuf_tensor("s_" + name, list(shape), dtype)
        n = 1
        for s in shape[1:]:
            n *= s
        self.sb_bytes += n * (4 if dtype == F32 else 2)
        b = Buf(self, name, h, dma)
        if dma:
            self.dma_bufs.append(b)
        return b

    def carve(self, name, ap, dma=False):
        b = Buf(self, name, ap, dma)
        if dma:
            self.dma_bufs.append(b)
        return b

    def dram(self, name, shape, dtype, kind="Internal"):
        h = self.nc.dram_tensor(name, list(shape), dtype, kind=kind)
        return Buf(self, name, h)

    def make_psum(self):
        for i in range(8):
            h = self.nc.alloc_psum_tensor(f"psb{i}", [128, 512], F32)
            self.psbanks.append(Buf(self, f"psb{i}", h))

    def psb(self):
        b = self.psbanks[self.psb_i % 6]
        self.psb_i += 1
        return b

    def _waits(self, e, reads, writes):
        need = {}

        def add(m):
            if m is None:
                return
            s, v = m
            k = s.num
            if k not in need or need[k][1] < v:
                need[k] = (s, v)

        for r in reads:
            add(r.last_write)
        for w in writes:
            add(w.last_write)
            for m in w.readers:
                add(m)
        for k, (s, v) in need.items():
            if e.name == "pe" and k == e.sem.num:
                continue
            if e.known.get(k, 0) >= v:
                continue
            e.obj.wait_ge(s, v)
            e.known[k] = v
            e.n_wait += 1

    def _mark(self, marker, reads, writes):
        for w in writes:
            w.last_write = marker
            w.readers = []
        for r in reads:
            if any(r is w for w in writes):
                continue
            r.readers.append(marker)
            if len(r.readers) > 48:
                best = {}
                for s, v in r.readers:
                    if s.num not in best or best[s.num][1] < v:
                        best[s.num] = (s, v)
                r.readers = list(best.values())

    _COST = {"pe": (60.0, 0.65), "act": (220.0, 0.6), "dve": (70.0, 1.0), "pool": (300.0, 2.2), "sp": (2500.0, 0.0)}

    def _vt(self, e, reads, writes, n):
        a, b = self._COST[e.name]
        st = e.free
        for r in reads:
            if r.ready > st:
                st = r.ready
        for w in writes:
            if w.ready > st:
                st = w.ready
            if w.rdone > st:
                st = w.rdone
        end = st + a + b * n
        e.free = end
        for w in writes:
            w.ready = end + 500.0
        for r in reads:
            if end > r.rdone:
                r.rdone = end
        self.last_end = end

    def op(self, en, fn, reads=(), writes=(), n=128):
        e = self.eng[en]
        self._vt(e, reads, writes, n)
        if e.count >= EPOCH:
            e.sem = self.nc.alloc_semaphore(f"e_{en}_{e.nsem}")
            e.nsem += 1
            e.count = 0
        self._waits(e, reads, writes)
        ins = fn(e.obj)
        e.count += 1
        e.n_instr += 1
        ins.then_inc(e.sem, 1)
        self._mark((e.sem, e.count), reads, writes)
        return ins

    def dma(self, pairs, reads=(), writes=(), sbuf=None, q="sp"):
        e = self.eng[q]
        self._vt(e, reads, writes, 0)
        self._waits(e, reads, writes)
        for o, i in pairs:
            e.obj.dma_start(out=o, in_=i).then_inc(sbuf.dma_sem, 16)
            sbuf.dma_cnt += 16
            e.n_instr += 1
        self._mark((sbuf.dma_sem, sbuf.dma_cnt), reads, writes)

    def barrier(self):
        for e in self.eng.values():
            for e2 in self.eng.values():
                if e2 is e or e2.count == 0:
                    continue
                if e.known.get(e2.sem.num, 0) < e2.count:
                    e.obj.wait_ge(e2.sem, e2.count)
                    e.known[e2.sem.num] = e2.count
            for b in self.dma_bufs:
                if b.dma_cnt and e.known.get(b.dma_sem.num, 0) < b.dma_cnt:
                    e.obj.wait_ge(b.dma_sem, b.dma_cnt)
                    e.known[b.dma_sem.num] = b.dma_cnt

    def stats(self):
        return {k: (e.n_instr, e.n_wait, e.nsem) for k, e in self.eng.items()}


class _Stop(Exception):
    pass


STOP = None
STOPN = None
import os as _os
BPRIO = float(_os.environ.get('BPRIO', '6000'))
PPRIO = float(_os.environ.get('PPRIO', '0'))
OPRIO = float(_os.environ.get('OPRIO', '5000'))
CPRIO = float(_os.environ.get('CPRIO', '4000'))
_NSTEP = [0]


def build(T, L, NS=1, dbg=None):
    try:
        return _build(T, L, NS, dbg)
    except _Stop as e:
        nc, fw = e.args
        fw.barrier()
        return nc, fw


def _build(T, L, NS=1, dbg=None):
    nc = bass.Bass("TRN2", target_bir_lowering=False)
    fw = Fw(nc)
    _NSTEP[0] = 0
    NTT = T // 128
    assert T % 128 == 0

    x_d = fw.dram("x", [T, D], F32, "ExternalInput")
    cv_d = fw.dram("cvec", [128, 8], F32, "ExternalInput")
    wada_d = fw.dram("wada", [L, D, 3 * D], F32, "ExternalInput")
    win_d = fw.dram("win", [L, D, PROJ], F32, "ExternalInput")
    wout_d = fw.dram("wout", [L, D, D], F32, "ExternalInput")
    pv_d = fw.dram("pv", [L, 128, NPV], F32, "ExternalInput")
    rows_d = fw.dram("rows", [L, 128, NROW], F32, "ExternalInput")
    bsp_d = fw.dram("bsp", [L, 1, 512], F32, "ExternalInput")
    bgate_d = fw.dram("bgate", [L, 1, D], F32, "ExternalInput")
    wsT_d = fw.dram("wsT", [L, 128, 512], F32, "ExternalInput")
    lora_d = fw.dram("lora", [L, 128, 384], F32, "ExternalInput")
    v1_d = fw.dram("v1r", [128, 96], F32, "ExternalInput")
    v2_d = fw.dram("v2r", [32, 384], F32, "ExternalInput")
    bg_d = fw.dram("biasg", [L, 128, 3840], F32, "ExternalInput")
    cst_d = fw.dram("cstf", [128, NCF], F32, "ExternalInput")
    cstb_d = fw.dram("cstb", [128, NCB], BF16, "ExternalInput")
    out_d = fw.dram("out", [T, D], F32, "ExternalOutput")
    x1_d = fw.dram("x1s", [T, D], F32) if L > 1 else None
    vf_d = fw.dram("vfs", [384, T], F32) if L > 1 else None

    fw.make_psum()

    cst = fw.sb("cstf_s", [128, NCF], F32, dma=True)
    cstb = fw.sb("cstb_s", [128, NCB], BF16, dma=True)
    win = fw.sb("win", [128, 8, PROJ], BF16)
    wout = fw.sb("woutb", [128, 8, D], BF16)
    rows = fw.sb("rows", [128, NROW], F32, dma=True)
    pv = fw.sb("pv", [128, NPV], F32, dma=True)
    eb = fw.sb("eb", [128, 5, 6, 128], BF16)
    wsT = fw.sb("wsTb", [128, 4, 128], BF16)
    lora = fw.sb("lorab", [128, 384], BF16)
    v1b = fw.sb("v1b", [128, 3, 32], BF16)
    v2b = fw.sb("v2b", [32, 384], BF16)
    bsp = fw.sb("bsp", [1, 1024], BF16)
    cvec = fw.sb("cvec", [128, 8], F32, dma=True)
    mod = fw.sb("mod", [128, 16], F32)
    dp = fw.sb("dp", [128, 16], F32)
    mhalf = fw.sb("mhalf", [128, 8], F32)
    hT = fw.sb("hT", [128, 8, 128], BF16)
    xs = [fw.sb(f"xs{i}", [128, D], F32, dma=True) for i in range(2)]
    xb = fw.sb("xb", [128, D], BF16)
    uT = [fw.sb(f"uT{i}", [128, 2, 128], BF16) for i in range(2)]
    gaT = [fw.sb(f"gaT{i}", [128, 2, 128], BF16) for i in range(2)]
    gbT = [fw.sb(f"gbT{i}", [128, 3, 128], BF16) for i in range(2)]
    gcT = [fw.sb(f"gcT{i}", [128, 3, 128], BF16) for i in range(2)]
    qT = [fw.sb(f"qT{i}", [128, 3, 128], BF16) for i in range(2)]
    mixT = [fw.sb(f"mixT{i}", [128, 8, 128], BF16) for i in range(2)]
    kT = fw.sb("kT", [128, 3, 768], BF16)
    vaug = fw.sb("vaug", [128, 6, 6, 68], BF16)
    pB = fw.sb("pB", [128, 10, 129], F32)
    sg1s = [fw.sb(f"sg1_{i}", [128, 256], F32) for i in range(2)]
    sg2 = fw.sb("sg2", [128, 256], F32)
    vnb = fw.sb("vnb", [128, 256], BF16)
    stA = fw.sb("stA", [128, 8], F32)
    stB = fw.sb("stB", [128, 24], F32)
    stC = fw.sb("stC", [128, 8], F32)
    stO = fw.sb("stO", [128, 16], F32)
    PT = fw.sb("PT", [128, 5, 2, 3, 128], BF16)
    att = fw.sb("att", [128, 384], BF16)
    Hs = fw.sb("Hs", [128, 3, 64], F32)
    Hb = fw.sb("Hb", [128, 3, 128], BF16)
    ATb = fw.sb("ATb", [128, 6, 4, 128], BF16)
    PQS = [fw.sb(f"PQS{h}", [128, 3, 128], BF16) for h in range(6)]
    arena = nc.alloc_sbuf_tensor("arena", [128, 9216], F32)
    fw.sb_bytes += 36864
    aap = arena.ap()
    stage = [fw.carve(f"stage{i}", aap[:, i * 4096:(i + 1) * 4096], dma=True) for i in range(2)]
    g1r = fw.carve("g1r", aap[:, 8192:9216], dma=True)
    _off = [0]

    def cv(name, nfree, dtype=F32, shape3=None, dma=False):
        n32 = nfree if dtype == F32 else (nfree + 1) // 2
        ap = aap[:, _off[0]:_off[0] + n32]
        _off[0] += n32
        assert _off[0] <= 9216, name
        if dtype == BF16:
            ap = ap.bitcast(BF16)
        if shape3:
            ap = ap.rearrange("p (a b) -> p a b", a=shape3[0])
        return fw.carve(name, ap, dma)

    xsB = cv("xsB", 1280, shape3=(10, 128))
    vfT = cv("vfT", 384, shape3=(3, 128), dma=True)
    tA = cv("tA", 384, shape3=(3, 128))
    tK = cv("tK", 384, shape3=(3, 128))
    tM = cv("tM", 384, shape3=(3, 128))
    tBv = cv("tBv", 384, shape3=(3, 128))
    tL = cv("tL", 384, shape3=(3, 128))
    tC = cv("tC", 384, shape3=(3, 128))
    tEi = cv("tEi", 384, shape3=(3, 128))
    tEn = cv("tEn", 384, shape3=(3, 128))
    tEe = cv("tEe", 384, shape3=(3, 128))
    t1 = cv("t1", 384, shape3=(3, 128))
    t2 = cv("t2", 384, shape3=(3, 128))
    ytm = cv("ytm", 384, shape3=(6, 64))
    ARb = cv("ARb", 768, BF16, shape3=(3, 256))
    BKb = cv("BKb", 768, BF16, shape3=(3, 256))
    bhb = cv("bhb", 384, BF16, shape3=(3, 128))
    khb = cv("khb", 384, BF16, shape3=(3, 128))
    vbf = cv("vbf", 384, BF16, shape3=(3, 128))
    twb = cv("twb", 128, BF16)
    Vtm = cv("Vtm", 384, BF16)
    Btm = cv("Btm", 384, BF16)
    Ktm = cv("Ktm", 384, BF16)
    Xb = cv("Xb", 384, BF16, shape3=(6, 64))
    Ub = cv("Ub", 384, BF16, shape3=(6, 64))
    ybf = cv("ybf", 384, BF16)
    vv1 = cv("vv1", 128, BF16)
    rkb = cv("rkb", 384, BF16, shape3=(3, 128))
    rks = cv("rks", 8)

    def mm(out, lhsT, rhs, start, stop, R, W, sgc=False):
        fw.op("pe", lambda e: e.matmul(out, lhsT=lhsT, rhs=rhs, start=start, stop=stop, skip_group_check=sgc), R, W,
              n=rhs.free_size())

    def tr(out, in_, ident, R, W):
        fw.op("pe", lambda e: e.transpose(out, in_, ident), R, W, n=128)

    def act(out, in_, func, R, W, bias=None, scale=None):
        kw = {}
        if bias is not None:
            kw["bias"] = bias
        if scale is not None:
            kw["scale"] = scale
        fw.op("act", lambda e: e.activation(out=out, in_=in_, func=func, **kw), R, W, n=out.free_size())

    def tt(en, out, in0, in1, op, R, W):
        fw.op(en, lambda e: e.tensor_tensor(out=out, in0=in0, in1=in1, op=op), R, W, n=out.free_size())

    def ts(en, out, in0, s1, s2, op0, op1, R, W):
        if s2 is None:
            fw.op(en, lambda e: e.tensor_scalar(out=out, in0=in0, scalar1=s1, scalar2=None, op0=op0), R, W, n=out.free_size())
        else:
            fw.op(en, lambda e: e.tensor_scalar(out=out, in0=in0, scalar1=s1, scalar2=s2, op0=op0, op1=op1), R, W, n=out.free_size())

    def stt(out, in0, scalar, in1, op0, op1, R, W):
        fw.op("dve", lambda e: e.scalar_tensor_tensor(out=out, in0=in0, scalar=scalar, in1=in1, op0=op0, op1=op1), R, W, n=out.free_size())

    def cp(en, out, in_, R, W):
        if en == "act":
            fw.op("act", lambda e: e.copy(out=out, in_=in_), R, W, n=out.free_size())
        else:
            fw.op(en, lambda e: e.tensor_copy(out=out, in_=in_), R, W, n=out.free_size())

    def rsqrt_pool(ap, buf):
        n = ap.shape[1]
        fw.op("pool", lambda e: e.tensor_tensor(out=ap, in0=ap, in1=mhalf[:, 0:n], op=ALU.pow), [buf, mhalf], [buf])

    def chk(stage):
        if STOP == stage:
            raise _Stop(nc, fw)

    fw.dma([(cst[:, :], cst_d[:, :])], reads=[], writes=[cst], sbuf=cst)
    fw.dma([(cstb[:, :], cstb_d[:, :])], reads=[], writes=[cstb], sbuf=cstb)
    identf = cst[:, F_ID:F_ID + 128]
    onesf = cst[:, F_ONE:F_ONE + 128]
    sel2f = cst[:, F_SEL:F_SEL + 2]
    identb = cstb[:, B_ID:B_ID + 128]
    M4 = cstb[:, B_M4:B_M4 + 512]
    MP0 = cstb[:, B_P0:B_P0 + 128]
    tril = cstb[:, B_TRIL:B_TRIL + 128]
    maskE = cstb[:, B_ME:B_ME + 256]
    scanm = cstb[:, B_SCAN:B_SCAN + 384]
    sel2b = cstb[:, B_SEL:B_SEL + 2]
    onesb = cstb[:, B_ONE:B_ONE + 128]
    negE = cstb[:, B_NEG:B_NEG + 256]
    fw.dma([(cvec[:, :], cv_d[:, :])], reads=[], writes=[cvec], sbuf=cvec)
    fw.op("pool", lambda e: e.memset(vaug[:, :, :, 64:68], 1.0), [], [vaug])
    fw.op("pool", lambda e: e.memset(mhalf[:, :], -0.5), [], [mhalf])
    C1 = -0.5 * float(np.exp(-0.5))

    for l in range(L):
        src_d = x_d if l == 0 else x1_d
        dst_d = out_d if l == L - 1 else x1_d
        fw.barrier()
        fw.dma([(pv[:, :], pv_d[l])], writes=[pv], sbuf=pv)
        fw.dma([(rows[:, :], rows_d[l])], writes=[rows], sbuf=rows)
        sgb = stage[0]
        fw.dma([(sgb[0:1, 0:512], bsp_d[l])], writes=[sgb], sbuf=sgb)
        cp("dve", bsp[0:1, 0:512], sgb[0:1, 0:512], [sgb], [bsp])
        tt("dve", bsp[0:1, 512:1024], sgb[0:1, 0:512], bsp[0:1, 0:512], ALU.subtract, [sgb, bsp], [bsp])
        ts("dve", dp[:, 0:6], pv[:, 10:16], 0.5, None, ALU.mult, None, [pv], [dp])
        ts("dve", dp[:, 6:9], pv[:, 25:28], 0.5, None, ALU.mult, None, [pv], [dp])
        ts("dve", dp[:, 9:12], pv[:, 19:22], -1.0, 1.0, ALU.mult, ALU.add, [pv], [dp])
        act(stB[:, 8:16], cvec[:, :], AF.Tanh, [cvec], [stB], scale=0.5)
        stt(stB[:, 0:8], stB[:, 8:16], 1.0, cvec[:, :], ALU.add, ALU.mult, [stB, cvec], [stB])
        ts("dve", stB[:, 0:8], stB[:, 0:8], 0.5, None, ALU.mult, None, [stB], [stB])
        modps = fw.psb()
        growps = [fw.psb(), fw.psb()]
        for ch in range(6):
            sg = stage[ch % 2]
            sgv = sg.h.rearrange("p (k n) -> p k n", k=8)
            fw.dma([(sgv[:, k, :], wada_d[l, k * 128:(k + 1) * 128, ch * 512:(ch + 1) * 512]) for k in range(8)],
                   writes=[sg], sbuf=sg)
            if ch < 4:
                for jj in range(4):
                    j = ch * 4 + jj
                    for k in range(8):
                        mm(modps[:, j:j + 1], sgv[:, k, jj * 128:(jj + 1) * 128], stB[:, k:k + 1],
                           k == 0, k == 7, [sg, stB], [modps])
            else:
                gp = growps[ch - 4]
                for k in range(8):
                    mm(gp[0:1, :], stB[:, k:k + 1], sgv[:, k, :], k == 0, k == 7, [sg, stB], [gp])
        tt("dve", mod[:, :], modps[:, 0:16], pv[:, 28:44], ALU.add, [modps, pv], [mod])
        ts("dve", mod[:, 8:16], mod[:, 8:16], 1.0, None, ALU.add, None, [mod], [mod])
        fw.dma([(g1r[0:1, :], bgate_d[l])], writes=[g1r], sbuf=g1r)
        for i in range(2):
            tt("dve", g1r[0:1, i * 512:(i + 1) * 512], g1r[0:1, i * 512:(i + 1) * 512], growps[i][0:1, :], ALU.add,
               [g1r, growps[i]], [g1r])
        ts("dve", g1r[0:1, :], g1r[0:1, :], 1.0, 0.5, ALU.add, ALU.mult, [g1r], [g1r])
        g1bc = [fw.psb(), fw.psb()]
        for i in range(2):
            mm(g1bc[i][:, :], onesf[0:1, :], g1r[0:1, i * 512:(i + 1) * 512], True, True, [cst, g1r], [g1bc[i]])
        for i in range(2):
            sg = stage[i]
            sgv = sg.h.rearrange("p (k n) -> p k n", k=8)
            fw.dma([(sgv[:, k, :], wout_d[l, k * 128:(k + 1) * 128, i * 512:(i + 1) * 512]) for k in range(8)],
                   writes=[sg], sbuf=sg)
            for k in range(8):
                tt("dve", wout[:, k, i * 512:(i + 1) * 512], sgv[:, k, :], g1bc[i][:, :], ALU.mult,
                   [sg, g1bc[i]], [wout])
        for ch in range(8):
            sg = stage[ch % 2]
            sgv = sg.h[:, 0:3968].rearrange("p (k n) -> p k n", k=8)
            fw.dma([(sgv[:, k, :], win_d[l, k * 128:(k + 1) * 128, ch * 496:(ch + 1) * 496]) for k in range(8)],
                   writes=[sg], sbuf=sg)
            en = "act" if ch % 2 == 0 else "dve"
            cp(en, win[:, :, ch * 496:(ch + 1) * 496], sgv[:, :, :], [sg], [win])
        sg = stage[0]
        fw.dma([(sg[:, 0:512], wsT_d[l])], writes=[sg], sbuf=sg)
        tt("dve", wsT[:, :, :], sg.h[:, 0:512].rearrange("p (g n) -> p g n", g=4),
           tril.unsqueeze(1).to_broadcast([128, 4, 128]), ALU.mult, [sg, cstb], [wsT])
        fw.dma([(sg[:, 512:896], lora_d[l])], writes=[sg], sbuf=sg)
        cp("dve", lora[:, :], sg[:, 512:896], [sg], [lora])
        if l > 0:
            fw.dma([(sg[:, 896:992], v1_d[:, :]), (sg[0:32, 1024:1408], v2_d[:, :])], writes=[sg], sbuf=sg)
            cp("dve", v1b[:, :, :], sg.h[:, 896:992].rearrange("p (a b) -> p a b", a=3), [sg], [v1b])
            cp("dve", v2b[:, :], sg[0:32, 1024:1408], [sg], [v2b])
        sg = stage[1]
        fw.dma([(sg[:, 0:3840], bg_d[l])], writes=[sg], sbuf=sg)
        for jj in range(5):
            src = sg.h[:, jj * 768:(jj + 1) * 768].rearrange("p (h q) -> p h q", h=6)
            if jj in (0, 4):
                mi = 0 if jj == 0 else 1
                tt("dve", src, src, maskE[:, mi * 128:(mi + 1) * 128].unsqueeze(1).to_broadcast([128, 6, 128]), ALU.mult,
                   [sg, cstb], [sg])
                tt("dve", eb[:, jj, :, :], src, negE[:, mi * 128:(mi + 1) * 128].unsqueeze(1).to_broadcast([128, 6, 128]),
                   ALU.add, [sg, cstb], [eb])
            else:
                cp("dve", eb[:, jj, :, :], src, [sg], [eb])
        fw.barrier()
        fw.op("dve", lambda e: e.memset(Hs[:, :, :], 0.0), [], [Hs])
        fw.op("dve", lambda e: e.memset(Hb[:, :, :], 0.0), [], [Hb])
        fw.op("pool", lambda e: e.memset(pB[:, :, 0:1], 0.0), [], [pB])

        def gen_proj(m):
            t0 = m * 128
            X = xs[m % 2]
            P = m % 2
            slot = m % 6
            fw.dma([(X[:, :], src_d[t0:t0 + 128, :])], reads=[src_d], writes=[X], sbuf=X)
            yield
            cp("act", xb[:, :], X[:, :], [X], [xb])
            yield
            for k2 in range(2):
                pt = fw.psb()
                ptb = pt.h.bitcast(BF16)
                for kk in range(4):
                    k = k2 * 4 + kk
                    tr(ptb[:, kk * 128:(kk + 1) * 128], xb[:, k * 128:(k + 1) * 128], identb, [xb, cstb], [pt])
                for kk in range(4):
                    k = k2 * 4 + kk
                    act(hT[:, k, :], ptb[:, kk * 128:(kk + 1) * 128], AF.Identity, [pt, mod], [hT],
                        bias=mod[:, k:k + 1], scale=mod[:, 8 + k:9 + k])
                yield

            def proj_fm(c0, n):
                pp = fw.psb()
                for j in range(n):
                    for k in range(8):
                        mm(pp[:, j * 128:(j + 1) * 128], win[:, k, c0 + j * 128:c0 + (j + 1) * 128], hT[:, k, :],
                           k == 0, k == 7, [win, hT], [pp])
                return pp

            def gate(pp, n, dst):
                act(dst[:, :, :], pp.h[:, 0:n * 128].rearrange("p (j t) -> p j t", j=n), AF.Tanh, [pp], [dst], scale=0.5)
                stt(dst[:, :, :], dst[:, :, :], 1.0, pp.h[:, 0:n * 128].rearrange("p (j t) -> p j t", j=n),
                    ALU.add, ALU.mult, [dst, pp], [dst])

            for c0, j0, n in ((768, 0, 4), (1280, 4, 4), (1792, 8, 2)):
                pp = proj_fm(c0, n)
                cp("act", pB[:, j0:j0 + n, 1:129], pp.h[:, 0:n * 128].rearrange("p (j t) -> p j t", j=n), [pp], [pB])
                yield
            pp = proj_fm(2048, 3)
            gate(pp, 3, gbT[P])
            yield
            pp = proj_fm(2432, 3)
            act(qT[P][:, :, :], pp.h[:, 0:384].rearrange("p (j t) -> p j t", j=3), AF.Copy, [pp], [qT[P]], scale=0.125)
            yield
            pp = proj_fm(2816, 3)
            cp("act", kT[:, :, slot * 128:(slot + 1) * 128], pp.h[:, 0:384].rearrange("p (j t) -> p j t", j=3), [pp], [kT])
            yield
            pp = fw.psb()
            for k in range(8):
                mm(pp[:, 0:384], hT[:, k, :], win[:, k, 3200:3584], k == 0, k == 7, [hT, win], [pp])
            cp("act", vaug[:, slot, :, 0:64], pp.h[:, 0:384].rearrange("p (h d) -> p h d", h=6), [pp], [vaug])
            yield
            pp = proj_fm(3584, 3)
            gate(pp, 3, gcT[P])
            yield
            pp = proj_fm(0, 2)
            cp("act", uT[P][:, :, :], pp.h[:, 0:256].rearrange("p (j t) -> p j t", j=2), [pp], [uT[P]])
            yield
            pp = proj_fm(512, 2)
            gate(pp, 2, gaT[P])
            yield
            pa = fw.psb()
            for k in range(8):
                mm(pa[:, 0:256], hT[:, k, :], win[:, k, 256:512], k == 0, k == 7, [hT, win], [pa])
            cp("act", sg1s[P][:, :], pa[:, 0:256], [pa], [sg1s[P]])
            yield

        def gen_A(m):
            P = m % 2
            sg1 = sg1s[P]
            av3 = sg1.h.rearrange("p (g c) -> p g c", g=4)
            s23 = sg2.h.rearrange("p (g c) -> p g c", g=4)
            fw.op("dve", lambda e: e.tensor_reduce(out=stA[:, 0:4], in_=av3, axis=AX.X, op=ALU.add), [sg1], [stA])
            ts("dve", stA[:, 0:4], stA[:, 0:4], 1.0 / 64, None, ALU.mult, None, [stA], [stA])
            tt("dve", av3, av3, stA[:, 0:4].unsqueeze(2).to_broadcast([128, 4, 64]), ALU.subtract, [sg1, stA], [sg1])
            yield
            act(sg2[:, :], sg1[:, :], AF.Square, [sg1], [sg2])
            fw.op("dve", lambda e: e.tensor_reduce(out=stA[:, 4:8], in_=s23, axis=AX.X, op=ALU.add), [sg2], [stA])
            ts("dve", stA[:, 4:8], stA[:, 4:8], 1.0 / 64, LN_EPS, ALU.mult, ALU.add, [stA], [stA])
            rsqrt_pool(stA[:, 4:8], stA)
            yield
            tt("dve", av3, av3, stA[:, 4:8].unsqueeze(2).to_broadcast([128, 4, 64]), ALU.mult, [sg1, stA], [sg1])
            tt("pool", sg1[:, :], sg1[:, :], rows[:, R_SG:R_SG + 256], ALU.mult, [sg1, rows], [sg1])
            tt("pool", vnb[:, :], sg1[:, :], rows[:, R_SB:R_SB + 256], ALU.add, [sg1, rows], [vnb])
            yield
            psa = fw.psb()
            for g in range(4):
                o = psa[(g % 2) * 64:(g % 2) * 64 + 64, (g // 2) * 128:(g // 2) * 128 + 128]
                mm(o, vnb[:, g * 64:(g + 1) * 64], wsT[:, g, :], True, False, [vnb, wsT], [psa])
                mm(o, onesb[0:1, 0:64], bsp[0:1, g * 128:(g + 1) * 128], False, False, [cstb, bsp], [psa])
                mm(o, onesb[0:1, 0:64], bsp[0:1, 512 + g * 128:512 + (g + 1) * 128], False, True, [cstb, bsp], [psa])
            s2v = sg2.h.rearrange("p (j t) -> p j t", j=2)
            tt("dve", s2v, psa.h[:, 0:256].rearrange("p (j t) -> p j t", j=2), uT[P][:, :, :], ALU.mult, [psa, uT[P]], [sg2])
            tt("pool", mixT[P][:, 0:2, :], s2v, gaT[P][:, :, :], ALU.mult, [sg2, gaT[P]], [mixT[P]])
            yield

        def gen_C(m):
            P = m % 2
            jlo = max(0, m - 4)
            po_ = fw.psbanks[6]
            o3 = po_.h[:, 0:408].rearrange("p (h d) -> p h d", h=6)
            for j in range(jlo, m + 1):
                jj = j - (m - 4)
                ks = (j % 6) * 128
                pS = [fw.psb(), fw.psb()]
                ebv = eb.h.rearrange("p j h q -> p j (h q)")
                for par in range(2):
                    mm(pS[par][:, 0:384], identb, ebv[:, jj, par * 384:(par + 1) * 384], True, False, [cstb, eb], [pS[par]])
                for h3 in range(3):
                    for par in range(2):
                        po = par * 64
                        mm(pS[par][:, h3 * 128:(h3 + 1) * 128], kT[po:po + 64, h3, ks:ks + 128],
                           qT[P][po:po + 64, h3, :], False, h3 == 2, [kT, qT[P]], [pS[par]])
                for par in range(2):
                    act(PT[:, jj, par, :, :], pS[par].h[:, 0:384].rearrange("p (h q) -> p h q", h=3), AF.Exp, [pS[par]], [PT])
                yield
            for h in range(6):
                par, h3 = h % 2, h // 2
                for j in range(jlo, m + 1):
                    jj = j - (m - 4)
                    mm(o3[:, h, :], PT[:, jj, par, h3, :], vaug[:, j % 6, h, :], j == jlo, j == m, [PT, vaug], [po_])
                if h % 2:
                    yield
            fw.op("dve", lambda e: e.reciprocal(out=stC[:, 0:6], in_=o3[:, :, 64]), [po_], [stC])
            tt("dve", att.h.rearrange("p (h d) -> p h d", h=6), o3[:, :, 0:64],
               stC[:, 0:6].unsqueeze(2).to_broadcast([128, 6, 64]), ALU.mult, [po_, stC], [att])
            yield
            pt = fw.psb()
            ptb = pt.h.bitcast(BF16)
            for jp in range(3):
                tr(ptb[:, jp * 128:(jp + 1) * 128], att[:, jp * 128:(jp + 1) * 128], identb, [att, cstb], [pt])
            tt("dve", mixT[P][:, 5:8, :], ptb[:, 0:384].rearrange("p (j t) -> p j t", j=3), gcT[P][:, :, :],
               ALU.mult, [pt, gcT[P]], [mixT[P]])
            yield

        def gen_B(m):
            P = m % 2
            t0 = m * 128
            W3 = pB[:, :, 1:129]
            Wp3 = pB[:, :, 0:128]
            tt("dve", xsB[:, :, :], Wp3, W3, ALU.subtract, [pB], [xsB])
            tt("dve", xsB[:, :, :], xsB[:, :, :], pv[:, 0:10].unsqueeze(2).to_broadcast([128, 10, 128]), ALU.mult,
               [xsB, pv], [xsB])
            tt("pool", xsB[:, :, :], xsB[:, :, :], W3, ALU.add, [xsB, pB], [xsB])
            cp("pool", pB[:, :, 0:1], pB[:, :, 128:129], [pB], [pB])
            yield
            r3 = xsB[:, 0:3, :]
            k3 = xsB[:, 3:6, :]
            v3 = xsB[:, 6:9, :]
            if L > 1 and l == 0:
                fw.dma([(vf_d[jp * 128:(jp + 1) * 128, t0:t0 + 128], xsB[:, 6 + jp, :])
                        for jp in range(3)], reads=[xsB], writes=[vf_d], sbuf=vfT)
            if l > 0:
                fw.dma([(vfT[:, jp, :], vf_d[jp * 128:(jp + 1) * 128, t0:t0 + 128])
                        for jp in range(3)], reads=[vf_d], writes=[vfT], sbuf=vfT)
            act(twb[0:64, :], xsB[0:64, 9, :], AF.Tanh, [xsB], [twb])
            cp("act", twb[64:128, :], xsB[64:128, 9, :], [xsB], [twb])
            yield
            pz = fw.psb()
            pa2 = fw.psb()
            for jp in range(3):
                mm(pz[:, jp * 128:(jp + 1) * 128], lora[0:64, jp * 128:(jp + 1) * 128], twb[0:64, :], True, True,
                   [lora, twb], [pz])
                mm(pa2[:, jp * 128:(jp + 1) * 128], lora[64:128, jp * 128:(jp + 1) * 128], twb[64:128, :], True, True,
                   [lora, twb], [pa2])
            for jp in range(3):
                act(tL[:, jp, :], pz[:, jp * 128:(jp + 1) * 128], AF.Tanh, [pz, dp], [tL], bias=dp[:, jp:jp + 1], scale=0.5)
                act(tA[:, jp, :], pa2[:, jp * 128:(jp + 1) * 128], AF.Tanh, [pa2, dp], [tA], bias=dp[:, 3 + jp:4 + jp], scale=0.5)
            ts("dve", tL[:, :, :], tL[:, :, :], C1, C1, ALU.mult, ALU.add, [tL], [tL])
            ts("dve", tA[:, :, :], tA[:, :, :], 0.5, 0.5, ALU.mult, ALU.add, [tA], [tA])
            yield
            if l > 0:
                cp("act", vbf[:, :, :], v3, [xsB], [vbf])
                pv1 = fw.psb()
                for jp in range(3):
                    mm(pv1[0:32, 0:128], v1b[:, jp, :], vbf[:, jp, :], jp == 0, jp == 2, [v1b, vbf], [pv1])
                cp("act", vv1[0:32, :], pv1[0:32, 0:128], [pv1], [vv1])
                pv2 = fw.psb()
                for jp in range(3):
                    mm(pv2[:, jp * 128:(jp + 1) * 128], v2b[0:32, jp * 128:(jp + 1) * 128], vv1[0:32, :], True, True,
                       [v2b, vv1], [pv2])
                for jp in range(3):
                    act(t1[:, jp, :], pv2[:, jp * 128:(jp + 1) * 128], AF.Tanh, [pv2, dp], [t1],
                        bias=dp[:, 6 + jp:7 + jp], scale=0.5)
                tt("dve", t2[:, :, :], vfT[:, :, :], v3, ALU.subtract, [vfT, xsB], [t2])
                stt(t2[:, :, :], t1[:, :, :], 1.0, t2[:, :, :], ALU.add, ALU.mult, [t1, t2], [t2])
                stt(v3, t2[:, :, :], 0.5, v3, ALU.mult, ALU.add, [t2, xsB], [xsB])
                yield
            cp("act", vbf[:, :, :], v3, [xsB], [vbf])
            tt("dve", tK[:, :, :], k3, pv[:, 16:19].unsqueeze(2).to_broadcast([128, 3, 128]), ALU.mult, [xsB, pv], [tK])
            act(rkb[:, :, :], tK[:, :, :], AF.Square, [tK], [rkb])
            yield
            pss = fw.psb()
            for jp in range(3):
                mm(pss[:, jp * 2:jp * 2 + 2], rkb[:, jp, :], sel2b, True, True, [rkb, cstb], [pss])
            ts("dve", stB[:, 0:6], pss[:, 0:6], 1e-24, None, ALU.max, None, [pss], [stB])
            rsqrt_pool(stB[:, 0:6], stB)
            yield
            cp("dve", ybf.h.rearrange("p (h d) -> p h d", h=6), stB[:, 0:6].unsqueeze(2).to_broadcast([128, 6, 64]), [stB], [ybf])
            prn = fw.psb()
            prnb = prn.h.bitcast(BF16)
            for jp in range(3):
                tr(prnb[:, jp * 128:(jp + 1) * 128], ybf[:, jp * 128:(jp + 1) * 128], identb, [ybf, cstb], [prn])
            tt("dve", tK[:, :, :], tK[:, :, :], prnb[:, 0:384].rearrange("p (a b) -> p a b", a=3), ALU.mult, [tK, prn], [tK])
            yield
            for jp in range(3):
                ts("dve", tM[:, jp, :], tA[:, jp, :], pv[:, 19 + jp:20 + jp], dp[:, 9 + jp:10 + jp], ALU.mult, ALU.add, [tA, pv, dp], [tM])
            tt("dve", tM[:, :, :], tM[:, :, :], k3, ALU.mult, [tM, xsB], [tM])
            tt("dve", tBv[:, :, :], tK[:, :, :], tA[:, :, :], ALU.mult, [tK, tA], [tBv])
            yield
            tt("pool", t1[:, :, :], r3, tM[:, :, :], ALU.mult, [xsB, tM], [t1])
            tt("pool", rkb[:, :, :], t1[:, :, :], pv[:, 22:25].unsqueeze(2).to_broadcast([128, 3, 128]), ALU.mult, [t1, pv], [rkb])
            prk = fw.psb()
            for jp in range(3):
                mm(prk[:, jp * 2:jp * 2 + 2], rkb[:, jp, :], sel2b, True, True, [rkb, cstb], [prk])
            cp("act", rks[:, 0:6], prk[:, 0:6], [prk], [rks])
            yield
            tCf = tC.h.rearrange("p a b -> p (a b)")
            tLf = tL.h.rearrange("p a b -> p (a b)")
            fw.op("dve", lambda e: e.tensor_tensor_scan(out=tCf, data0=scanm, data1=tLf, initial=0.0,
                                                        op0=ALU.mult, op1=ALU.add), [cstb, tL], [tC])
            act(tEi[:, :, :], tC[:, :, :], AF.Exp, [tC], [tEi])
            act(tEn[:, :, :], tC[:, :, :], AF.Exp, [tC], [tEn], scale=-1.0)
            tt("dve", t1[:, :, :], tC[:, :, :], tL[:, :, :], ALU.subtract, [tC, tL], [t1])
            act(t1[:, :, :], t1[:, :, :], AF.Exp, [t1], [t1])
            yield
            ei4 = tEi.h.rearrange("p a (c t) -> p a c t", c=2)
            tt("dve", tEe.h.rearrange("p a (c t) -> p a c t", c=2), tEn.h.rearrange("p a (c t) -> p a c t", c=2),
               ei4[:, :, :, 63:64].to_broadcast([128, 3, 2, 64]), ALU.mult, [tEn, tEi], [tEe])
            AR4 = ARb.h.rearrange("p a (w t) -> p a w t", w=2)
            BK4 = BKb.h.rearrange("p a (w t) -> p a w t", w=2)
            stt(AR4[:, :, 0, :], tK[:, :, :], -1.0, t1[:, :, :], ALU.mult, ALU.mult, [tK, t1], [ARb])
            tt("dve", AR4[:, :, 1, :], r3, tEi[:, :, :], ALU.mult, [xsB, tEi], [ARb])
            tt("dve", BK4[:, :, 0, :], tBv[:, :, :], tEn[:, :, :], ALU.mult, [tBv, tEn], [BKb])
            tt("dve", BK4[:, :, 1, :], tM[:, :, :], tEn[:, :, :], ALU.mult, [tM, tEn], [BKb])
            yield
            tt("dve", bhb[:, :, :], tBv[:, :, :], tEe[:, :, :], ALU.mult, [tBv, tEe], [bhb])
            tt("pool", khb[:, :, :], tM[:, :, :], tEe[:, :, :], ALU.mult, [tM, tEe], [khb])
            for h in range(6):
                jp, po = h // 2, (h % 2) * 64
                pA = fw.psb()
                mm(pA[:, 0:256], BKb[po:po + 64, jp, 0:128], ARb[po:po + 64, jp, :], True, True, [BKb, ARb], [pA])
                mm(pA[:, 256:512], BKb[po:po + 64, jp, 128:256], ARb[po:po + 64, jp, :], True, True, [BKb, ARb], [pA])
                tt("dve", ATb[:, h, 1:4, :], pA.h[:, 128:512].rearrange("p (w t) -> p w t", w=3),
                   M4[:, 128:512].rearrange("p (w t) -> p w t", w=3), ALU.mult, [pA, cstb], [ATb])
                tt("dve", PQS[h][:, 1, :], pA[:, 0:128], M4[:, 0:128], ALU.mult, [pA, cstb], [PQS[h]])
                pP = fw.psb()
                mm(pP[:, 0:128], ARb[po:po + 64, jp, 0:128], BKb[po:po + 64, jp, 0:128], True, True, [ARb, BKb], [pP])
                tt("dve", PQS[h][:, 0, :], pP[:, 0:128], MP0, ALU.mult, [pP, cstb], [PQS[h]])
                cp("act", PQS[h][:, 2, :], identb, [cstb], [PQS[h]])
                if h % 2:
                    yield
            yield
            for srcb, dstb in ((vbf, Vtm), (bhb, Btm), (khb, Ktm)):
                pt = fw.psb()
                ptb = pt.h.bitcast(BF16)
                for jp in range(3):
                    tr(ptb[:, jp * 128:(jp + 1) * 128], srcb[:, jp, :], identb, [srcb, cstb], [pt])
                cp("act", dstb[:, :], ptb[:, 0:384], [pt], [dstb])
            yield
            for i in range(6):
                for h in range(6):
                    B_ = PQS[h]
                    pp_ = fw.psb()
                    if i < 5:
                        mm(pp_[:, 0:128], B_[:, 1, :], B_[:, 0, :], True, True, [B_], [pp_])
                        mm(pp_[:, 256:384], identb, B_[:, 2, :], True, False, [cstb, B_], [pp_], sgc=True)
                        mm(pp_[:, 128:384], B_[:, 0, :], B_[:, 1:3, :], False, True, [B_], [pp_], sgc=True)
                        cp("act" if (h + i) % 2 else "dve", B_[:, :, :],
                           pp_.h[:, 0:384].rearrange("p (w t) -> p w t", w=3), [pp_], [B_])
                    else:
                        mm(pp_[:, 256:384], identb, B_[:, 2, :], True, False, [cstb, B_], [pp_])
                        mm(pp_[:, 256:384], B_[:, 0, :], B_[:, 2, :], False, True, [B_], [pp_])
                        cp("act" if (h + i) % 2 else "dve", B_[:, 2, :], pp_[:, 256:384], [pp_], [B_])
                    if h % 2:
                        yield
            pY = fw.psbanks[7]
            pY3 = pY.h[:, 0:384].rearrange("p (h d) -> p h d", h=6)
            for c in range(2):
                cs = slice(c * 64, c * 64 + 64)
                pX = fw.psb()
                pX3 = pX.h[:, 0:384].rearrange("p (h d) -> p h d", h=6)
                for jp in range(3):
                    mm(pX[cs, jp * 128:(jp + 1) * 128], ARb[:, jp, c * 64:c * 64 + 64], Hb[:, jp, :], True, False,
                       [ARb, Hb], [pX])
                    for h in (2 * jp, 2 * jp + 1):
                        mm(pX3[cs, h, :], ATb[cs, h, 2, cs], Vtm[cs, h * 64:(h + 1) * 64], False, h == 2 * jp + 1,
                           [ATb, Vtm], [pX])
                cp("act", Xb[cs, :, :], pX3[cs, :, :], [pX], [Xb])
                yield
                pU = fw.psb()
                pU3 = pU.h[:, 0:384].rearrange("p (h d) -> p h d", h=6)
                for h in range(6):
                    mm(pU3[cs, h, :], PQS[h][cs, 2, cs], Xb[cs, h, :], True, True, [PQS[h], Xb], [pU])
                cp("act", Ub[cs, :, :], pU3[cs, :, :], [pU], [Ub])
                yield
                for jp in range(3):
                    mm(pY[cs, jp * 128:(jp + 1) * 128], ARb[:, jp, 128 + c * 64:128 + c * 64 + 64], Hb[:, jp, :], True, False,
                       [ARb, Hb], [pY])
                    for h in (2 * jp, 2 * jp + 1):
                        mm(pY3[cs, h, :], ATb[cs, h, 1, cs], Ub[cs, h, :], False, False, [ATb, Ub], [pY])
                        mm(pY3[cs, h, :], ATb[cs, h, 3, cs], Vtm[cs, h * 64:(h + 1) * 64], False, h == 2 * jp + 1,
                           [ATb, Vtm], [pY])
                pH = fw.psb()
                for h in range(6):
                    jp, po = h // 2, (h % 2) * 64
                    o = pH[po:po + 64, jp * 64:(jp + 1) * 64]
                    mm(o, Ktm[cs, h * 64:(h + 1) * 64], Vtm[cs, h * 64:(h + 1) * 64], True, False, [Ktm, Vtm], [pH])
                    mm(o, Btm[cs, h * 64:(h + 1) * 64], Ub[cs, h, :], False, True, [Btm, Ub], [pH])
                for jp in range(3):
                    stt(Hs[:, jp, :], Hs[:, jp, :], tEi[:, jp, c * 64 + 63:c * 64 + 64], pH[:, jp * 64:(jp + 1) * 64],
                        ALU.mult, ALU.add, [Hs, tEi, pH], [Hs])
                cp("act", Hb[0:64, :, 0:64], Hs[0:64, :, :], [Hs], [Hb])
                cp("act", Hb[64:128, :, 64:128], Hs[64:128, :, :], [Hs], [Hb])
                yield
            fw.op("dve", lambda e: e.tensor_reduce(out=stB[:, 8:14], in_=pY3, axis=AX.X, op=ALU.add), [pY], [stB])
            ts("dve", stB[:, 8:14], stB[:, 8:14], 1.0 / 64, None, ALU.mult, None, [stB], [stB])
            tt("dve", ytm[:, :, :], pY3, stB[:, 8:14].unsqueeze(2).to_broadcast([128, 6, 64]), ALU.subtract, [pY, stB], [ytm])
            t1y = t1.h.rearrange("p a b -> p (a b)").rearrange("p (h d) -> p h d", h=6)
            act(t1y, ytm[:, :, :], AF.Square, [ytm], [t1])
            fw.op("dve", lambda e: e.tensor_reduce(out=stB[:, 16:22], in_=t1y, axis=AX.X, op=ALU.add), [t1], [stB])
            ts("dve", stB[:, 16:22], stB[:, 16:22], 1.0 / 64, GN_EPS, ALU.mult, ALU.add, [stB], [stB])
            rsqrt_pool(stB[:, 16:22], stB)
            yield
            tt("dve", ytm[:, :, :], ytm[:, :, :], stB[:, 16:22].unsqueeze(2).to_broadcast([128, 6, 64]), ALU.mult, [ytm, stB], [ytm])
            yf = ytm.h.rearrange("p a b -> p (a b)")
            tt("dve", yf, yf, rows[:, R_XG:R_XG + 384], ALU.mult, [ytm, rows], [ytm])
            tt("dve", yf, yf, rows[:, R_XB:R_XB + 384], ALU.add, [ytm, rows], [ytm])
            tt("pool", t1y, Vtm.h.rearrange("p (h d) -> p h d", h=6), rks[:, 0:6].unsqueeze(2).to_broadcast([128, 6, 64]),
               ALU.mult, [Vtm, rks], [t1])
            tt("dve", ybf.h.rearrange("p (h d) -> p h d", h=6), ytm[:, :, :], t1y, ALU.add, [ytm, t1], [ybf])
            yield
            pt = fw.psb()
            ptb = pt.h.bitcast(BF16)
            for jp in range(3):
                tr(ptb[:, jp * 128:(jp + 1) * 128], ybf[:, jp * 128:(jp + 1) * 128], identb, [ybf, cstb], [pt])
            tt("dve", mixT[P][:, 2:5, :], ptb[:, 0:384].rearrange("p (j t) -> p j t", j=3), gbT[P][:, :, :],
               ALU.mult, [pt, gbT[P]], [mixT[P]])
            yield

        def gen_out(m):
            P = m % 2
            t0 = m * 128
            z = xs[m % 2]
            py = [fw.psb(), fw.psb()]
            for i in range(2):
                for k in range(8):
                    mm(py[i][:, :], mixT[P][:, k, :], wout[:, k, i * 512:(i + 1) * 512], k == 0, k == 7, [mixT[P], wout], [py[i]])
            for i in range(2):
                stt(z[:, i * 512:(i + 1) * 512], z[:, i * 512:(i + 1) * 512], float(ALPHA), py[i][:, :],
                    ALU.mult, ALU.add, [z, py[i]], [z])
                fw.op("dve", lambda e: e.bn_stats(out=stO[:, 6 * i:6 + 6 * i], in_=z[:, i * 512:(i + 1) * 512]), [z], [stO])
            yield
            fw.op("dve", lambda e: e.bn_aggr(out=stO[:, 12:14], in_=stO[:, 0:12]), [stO], [stO])
            ts("dve", stO[:, 13:14], stO[:, 13:14], LN_EPS, None, ALU.add, None, [stO], [stO])
            rsqrt_pool(stO[:, 13:14], stO)
            stt(stO[:, 14:15], stO[:, 12:13], -1.0, stO[:, 13:14], ALU.mult, ALU.mult, [stO], [stO])
            act(z[:, :], z[:, :], AF.Identity, [z, stO], [z], bias=stO[:, 14:15], scale=stO[:, 13:14])
            yield
            tt("pool", z[:, :], z[:, :], rows[:, R_LNG:R_LNG + 1024], ALU.mult, [z, rows], [z])
            tt("pool", z[:, :], z[:, :], rows[:, R_LNB:R_LNB + 1024], ALU.add, [z, rows], [z])
            fw.dma([(dst_d[t0:t0 + 128, :], z[:, :])], reads=[z], writes=[dst_d], sbuf=z)
            yield

        def run_layer():
            mk = {"proj": gen_proj, "B": gen_B, "C": gen_C, "A": gen_A, "out": gen_out}
            done, bfirst, act_ = set(), set(), []
            nxt = {n: 0 for n in mk}

            def ok(n, k):
                return k < 0 or (n, k) in done

            def eligible(n, k):
                if n == "proj":
                    return (ok("proj", k - 1) and (k - 1 < 0 or (k - 1) in bfirst) and ok("A", k - 2)
                            and ok("B", k - 2) and ok("C", k - 2) and ok("out", k - 2))
                if n == "out":
                    return ok("A", k) and ok("B", k) and ok("C", k) and ok("out", k - 1)
                return ok("proj", k) and ok(n, k - 1) and ok("out", k - 2)

            while len(done) < 5 * NTT:
                for n in mk:
                    k = nxt[n]
                    if k < NTT and eligible(n, k):
                        act_.append([n, k, mk[n](k), 0.0, 0])
                        nxt[n] += 1
                i = min(range(len(act_)), key=lambda j: act_[j][3] - (BPRIO if act_[j][0] == 'B' else 0.0) - (PPRIO if act_[j][0] == 'proj' else 0.0) - (OPRIO if act_[j][0] == 'out' else 0.0) - (CPRIO if act_[j][0] == 'C' else 0.0))
                _NSTEP[0] += 1
                if STOPN is not None and _NSTEP[0] > STOPN:
                    raise _Stop(nc, fw)
                ent = act_[i]
                try:
                    next(ent[2])
                    ent[3] = fw.last_end
                    ent[4] += 1
                    if ent[0] == "B" and ent[4] == 1:
                        bfirst.add(ent[1])
                except StopIteration:
                    done.add((ent[0], ent[1]))
                    act_.pop(i)

        run_layer()
    fw.barrier()
    return nc, fw


def _consts():
    cf = np.zeros((128, NCF), np.float32)
    cb = np.zeros((128, NCB), np.float32)
    s = np.arange(128)[:, None]
    t = np.arange(128)[None, :]
    same = (s // 64) == (t // 64)
    cf[:, F_ID:F_ID + 128] = np.eye(128)
    cf[:, F_ONE:F_ONE + 128] = 1.0
    cf[:, F_SEL] = (np.arange(128) < 64)
    cf[:, F_SEL + 1] = (np.arange(128) >= 64)
    cb[:, B_ID:B_ID + 128] = np.eye(128)
    mst = (same & (t > s)).astype(np.float32)
    minc = (same & (t >= s)).astype(np.float32)
    cb[:, B_M4:B_M4 + 512] = np.concatenate([mst, minc, mst, minc], 1)
    cb[:, B_P0:B_P0 + 128] = (same & (t < s)).astype(np.float32)
    cb[:, B_TRIL:B_TRIL + 128] = (s <= t).astype(np.float32)
    cb[:, B_ME:B_ME + 128] = ((s // 64) >= (t // 64)).astype(np.float32)
    cb[:, B_ME + 128:B_ME + 256] = ((s // 64) <= (t // 64)).astype(np.float32)
    sm = np.ones(384, np.float32)
    sm[::64] = 0.0
    cb[:, B_SCAN:B_SCAN + 384] = sm[None, :]
    cb[:, B_SEL] = (np.arange(128) < 64)
    cb[:, B_SEL + 1] = (np.arange(128) >= 64)
    cb[:, B_NEG:B_NEG + 256] = (cb[:, B_ME:B_ME + 256] - 1.0) * 30000.0
    cb[:, B_ONE:B_ONE + 128] = 1.0
    return cf, cb.astype(ml_dtypes.bfloat16)


def _cols(v, n):
    return np.ascontiguousarray(np.asarray(v, np.float32).reshape(n, 128).T)


_CACHE = {}


def kernel(x, c, w_ada, b_ada, w_in, sgu_ln_g, sgu_ln_b, w_spatial, b_spatial, mu_shift, w_decay0, w_decay2,
           a0, a2, k_k, k_a, r_k, lnx_g, lnx_b, v0, v1, v2, rel_bias, w_out, ln_g, ln_b, _dbg=None, _ncores=None):
    x = np.asarray(x, np.float32)
    B, T, _ = x.shape
    L = int(np.asarray(w_in).shape[0])
    f = lambda a: np.asarray(a, np.float32)
    pvs = np.zeros((L, 128, NPV), np.float32)
    rws = np.zeros((L, 128, NROW), np.float32)
    for l in range(L):
        pvs[l, :, 0:10] = _cols(f(mu_shift)[l], 10)
        pvs[l, :, 10:13] = _cols(f(w_decay0)[l], 3)
        pvs[l, :, 13:16] = _cols(f(a0)[l], 3)
        pvs[l, :, 16:19] = _cols(f(k_k)[l], 3)
        pvs[l, :, 19:22] = _cols(f(k_a)[l], 3)
        pvs[l, :, 22:25] = _cols(f(r_k)[l].reshape(-1), 3)
        if l > 0:
            pvs[l, :, 25:28] = _cols(f(v0)[l - 1], 3)
        pvs[l, :, 28:52] = _cols(f(b_ada)[l], 24)
        rws[l, :, R_LNG:R_LNG + 1024] = f(ln_g)[l][None]
        rws[l, :, R_LNB:R_LNB + 1024] = f(ln_b)[l][None]
        rws[l, :, R_SG:R_SG + 256] = f(sgu_ln_g)[l][None]
        rws[l, :, R_SB:R_SB + 256] = f(sgu_ln_b)[l][None]
        rws[l, :, R_XG:R_XG + 384] = f(lnx_g)[l][None]
        rws[l, :, R_XB:R_XB + 384] = f(lnx_b)[l][None]
    bsp = np.ascontiguousarray(f(b_spatial).reshape(L, 1, 512))
    bgate = np.ascontiguousarray(f(b_ada)[:, None, 2048:3072])
    wsT = np.ascontiguousarray(np.transpose(f(w_spatial), (0, 3, 1, 2)).reshape(L, 128, 512))
    lora = np.ascontiguousarray(np.concatenate([f(w_decay2), f(a2)], axis=1))
    if L > 1:
        v1r = np.ascontiguousarray(f(v1)[0].reshape(3, 128, 32).transpose(1, 0, 2).reshape(128, 96))
        v2r = np.ascontiguousarray(f(v2)[0])
    else:
        v1r = np.zeros((128, 96), np.float32)
        v2r = np.zeros((32, 384), np.float32)
    kk_ = np.arange(128)[:, None, None]
    jj_ = np.arange(5)[None, :, None]
    qq_ = np.arange(128)[None, None, :]
    idx = np.clip(qq_ - kk_ + 128 * (4 - jj_), -256, 256) + 256
    bg = f(rel_bias)[:, [0, 2, 4, 1, 3, 5]][:, :, idx]
    bg = np.ascontiguousarray(np.transpose(bg, (0, 2, 3, 1, 4)).reshape(L, 128, 3840))
    cstf, cstb = _consts()
    key = (T, L, tuple(sorted(_dbg.items())) if _dbg else None)
    nc, fw = build(T, L, dbg=_dbg)
    ncores = _ncores or B
    in_maps = []
    for b in range(ncores):
        in_maps.append({
            "x": np.ascontiguousarray(x[b]), "cvec": _cols(f(c)[b], 8),
            "wada": f(w_ada), "win": f(w_in), "wout": f(w_out),
            "pv": pvs, "rows": rws, "bsp": bsp, "bgate": bgate, "wsT": wsT, "lora": lora,
            "v1r": v1r, "v2r": v2r, "biasg": bg, "cstf": cstf, "cstb": cstb,
        })
    res = run_bass_kernel_spmd(nc, in_maps, core_ids=list(range(ncores)))
    out = np.stack([np.asarray(r["out"], np.float32) for r in res.results], 0)
    if _dbg:
        return out, res.results
    return out
```
